# Optimizing a Trainium2 kernel written in Bass

```python
import math
import jax, jax.numpy as jnp
from jax import lax
import numpy as np

D_MODEL = 1024
BATCH = 4
SEQ = 4096
DEPTH = 1

D_MIX = D_MODEL
D_ATTN = D_MIX // 2
D_CONV = D_MIX - D_ATTN
HEAD_DIM = 64
N_HEADS = D_ATTN // HEAD_DIM
N_KV_HEADS = 2
GQA_GROUP = N_HEADS // N_KV_HEADS
WINDOW = 128
BLOCK = 128
NUM_BUCKETS = 32
MAX_DISTANCE = 128
CONV_WIDTH = 31
CONV_GROUPS = 8
D_Q = N_HEADS * HEAD_DIM
D_KV = N_KV_HEADS * HEAD_DIM
D_IN = D_Q + 2 * D_KV + D_ATTN + 2 * D_CONV + D_CONV
EPS = 1e-6
NEG_INF = -1e30

kernel_name = "hymba_conformer_swa_sink_block"


def rmsnorm(x, w, eps=EPS):
    xf = x.astype(jnp.float32)
    y = xf * lax.rsqrt(jnp.mean(xf * xf, axis=-1, keepdims=True) + eps)
    return (y * w.astype(jnp.float32)).astype(x.dtype)


def layernorm(x, w, b, eps=1e-5):
    xf = x.astype(jnp.float32)
    mu = jnp.mean(xf, axis=-1, keepdims=True)
    var = jnp.mean(jnp.square(xf - mu), axis=-1, keepdims=True)
    y = (xf - mu) * lax.rsqrt(var + eps)
    return (y * w.astype(jnp.float32) + b.astype(jnp.float32)).astype(x.dtype)


def t5_causal_bucket(dist):
    n = jnp.maximum(dist, 0)
    max_exact = NUM_BUCKETS // 2
    nf = jnp.maximum(n, 1).astype(jnp.float32)
    large = max_exact + (jnp.log(nf / max_exact) / math.log(MAX_DISTANCE / max_exact)
                         * (NUM_BUCKETS - max_exact)).astype(jnp.int32)
    large = jnp.minimum(large, NUM_BUCKETS - 1)
    return jnp.where(n < max_exact, n, large)


def sliding_window_attention(q, k, v, rel_bias, sinks):
    B, S = q.shape[0], q.shape[1]
    nb = S // BLOCK
    scale = HEAD_DIM ** -0.5
    qb = q.reshape(B, nb, BLOCK, N_KV_HEADS, GQA_GROUP, HEAD_DIM)
    pad = ((0, 0), (BLOCK, 0), (0, 0), (0, 0))
    kp = jnp.pad(k, pad).reshape(B, nb + 1, BLOCK, N_KV_HEADS, HEAD_DIM)
    vp = jnp.pad(v, pad).reshape(B, nb + 1, BLOCK, N_KV_HEADS, HEAD_DIM)
    kwin = jnp.concatenate([kp[:, :-1], kp[:, 1:]], axis=2)
    vwin = jnp.concatenate([vp[:, :-1], vp[:, 1:]], axis=2)

    logits = jnp.einsum('bnqkgd,bnskd->bnkgqs', qb, kwin,
                        preferred_element_type=jnp.float32) * scale
    qi = jnp.arange(BLOCK)[:, None]
    sj = jnp.arange(2 * BLOCK)[None, :]
    dist = qi + BLOCK - sj
    band = (dist >= 0) & (dist < WINDOW)
    bias = rel_bias.astype(jnp.float32)[t5_causal_bucket(dist)]
    bias = bias.transpose(2, 0, 1).reshape(N_KV_HEADS, GQA_GROUP, BLOCK, 2 * BLOCK)
    valid_key = (jnp.arange(nb)[:, None] > 0) | (sj >= BLOCK)
    mask = band[None, :, :] & valid_key[:, None, :]
    logits = jnp.where(mask[None, :, None, None, :, :], logits + bias[None, None], NEG_INF)

    sink = sinks.astype(jnp.float32).reshape(1, 1, N_KV_HEADS, GQA_GROUP, 1, 1)
    m = jnp.maximum(jnp.max(logits, axis=-1, keepdims=True), sink)
    p = jnp.exp(logits - m)
    denom = jnp.sum(p, axis=-1, keepdims=True) + jnp.exp(sink - m)
    probs = (p / denom).astype(v.dtype)
    out = jnp.einsum('bnkgqs,bnskd->bnqkgd', probs, vwin)
    return out.reshape(B, S, D_ATTN)


def conformer_conv(u_glu, dw_w, dw_b, ln_w, ln_b):
    a, g = jnp.split(u_glu, 2, axis=-1)
    u = a * jax.nn.sigmoid(g)
    y = lax.conv_general_dilated(
        u, dw_w[:, None, :].astype(u.dtype), window_strides=(1,),
        padding=[(CONV_WIDTH - 1, 0)], dimension_numbers=('NWC', 'WIO', 'NWC'),
        feature_group_count=D_CONV) + dw_b
    y = layernorm(y, ln_w, ln_b)
    return jax.nn.silu(y)


def hybrid_layer(x, norm_w, w_in, q_norm_w, k_norm_w, sinks, dw_w, dw_b,
                 ln_w, ln_b, w_out, rel_bias):
    B, S, _ = x.shape
    h = rmsnorm(x, norm_w)
    proj = h @ w_in
    splits = np.cumsum([D_Q, D_KV, D_KV, D_ATTN, 2 * D_CONV])
    q, k, v, z_attn, u_glu, z_conv = jnp.split(proj, splits, axis=-1)

    q = rmsnorm(q.reshape(B, S, N_KV_HEADS, GQA_GROUP, HEAD_DIM), q_norm_w)
    k = rmsnorm(k.reshape(B, S, N_KV_HEADS, HEAD_DIM), k_norm_w)
    v = v.reshape(B, S, N_KV_HEADS, HEAD_DIM)
    y_attn = sliding_window_attention(q, k, v, rel_bias, sinks) * jax.nn.silu(z_attn)

    y_conv = conformer_conv(u_glu, dw_w, dw_b, ln_w, ln_b) * jax.nn.silu(z_conv)

    y = jnp.concatenate([y_attn, y_conv], axis=-1) @ w_out
    return x + y


def setup_inputs(seed: int = 0) -> dict:
    key = jax.random.key(seed)
    ks = jax.random.split(key, 12)
    L = DEPTH
    f = jnp.float32
    return {
        "x": jax.random.normal(ks[0], (BATCH, SEQ, D_MODEL), f),
        "norm_w": 1.0 + 0.05 * jax.random.normal(ks[1], (L, D_MODEL), f),
        "w_in": jax.random.normal(ks[2], (L, D_MODEL, D_IN), f) * D_MODEL ** -0.5,
        "q_norm_w": 1.0 + 0.05 * jax.random.normal(ks[3], (L, HEAD_DIM), f),
        "k_norm_w": 1.0 + 0.05 * jax.random.normal(ks[4], (L, HEAD_DIM), f),
        "sinks": 0.5 * jax.random.normal(ks[5], (L, N_HEADS), f),
        "dw_w": jax.random.normal(ks[6], (L, CONV_WIDTH, D_CONV), f) * CONV_WIDTH ** -0.5,
        "dw_b": 0.02 * jax.random.normal(ks[7], (L, D_CONV), f),
        "ln_w": 1.0 + 0.05 * jax.random.normal(ks[8], (L, D_CONV), f),
        "ln_b": 0.02 * jax.random.normal(ks[9], (L, D_CONV), f),
        "w_out": jax.random.normal(ks[10], (L, D_MIX, D_MODEL), f) * D_MIX ** -0.5,
        "rel_bias": 0.5 * jax.random.normal(ks[11], (NUM_BUCKETS, N_HEADS), f),
    }


def reference(x, norm_w, w_in, q_norm_w, k_norm_w, sinks, dw_w, dw_b,
              ln_w, ln_b, w_out, rel_bias):
    for l in range(DEPTH):
        x = hybrid_layer(x, norm_w[l], w_in[l], q_norm_w[l], k_norm_w[l], sinks[l],
                         dw_w[l], dw_b[l], ln_w[l], ln_b[l], w_out[l], rel_bias)
    return x
```

```python
import math
from contextlib import ExitStack

import numpy as np
import ml_dtypes

import concourse.bass as bass
import concourse.mybir as mybir
from concourse.bass_utils import run_bass_kernel_spmd

F32 = mybir.dt.float32
BF16 = mybir.dt.bfloat16
ALU = mybir.AluOpType
ACTF = mybir.ActivationFunctionType

ENGS = ("pe", "act", "dve", "pool", "sp")
NEG = -30000.0
NCORES = 8
SEQ_PER_CORE = 2048
HALO = 128
NTOK = SEQ_PER_CORE + HALO
DEBUG = False


class Plan:
    def __init__(self):
        self.streams = {e: [] for e in ENGS}
        self.count = {}
        self.waited = {e: {} for e in ENGS}
        self.last_w = {}
        self.readers = {}

    def _need(self, eng, dep, waits):
        if dep is None:
            return
        sk, val = dep
        if eng == "pe" and sk == ("eng", "pe"):
            return
        if self.waited[eng].get(sk, 0) >= val:
            return
        self.waited[eng][sk] = val
        for w in waits:
            if w[0] == sk:
                w[1] = max(w[1], val)
                return
        waits.append([sk, val])

    def op(self, eng, fn, reads=(), writes=(), dma_key=None):
        waits = []
        for k in reads:
            self._need(eng, self.last_w.get(k), waits)
        for k in writes:
            self._need(eng, self.last_w.get(k), waits)
            for r in self.readers.get(k, ()):
                self._need(eng, r, waits)
        if dma_key is not None:
            sk = ("dma", dma_key)
            inc = 16
        else:
            sk = ("eng", eng)
            inc = 1
        val = self.count.get(sk, 0) + inc
        self.count[sk] = val
        done = (sk, val)
        for k in reads:
            self.readers.setdefault(k, []).append(done)
        for k in writes:
            self.last_w[k] = done
            self.readers[k] = []
        self.streams[eng].append((waits, fn, sk, inc))
        return done

    def final_waits(self, eng):
        waits = []
        for sk, val in self.count.items():
            self._need(eng, (sk, val), waits)
        self.streams[eng].append((waits, None, None, 0))

    def emit(self, nc, sems):
        with nc.Block() as block:
            class Rec:
                def __init__(self, e):
                    self._e, self.first = e, None

                def __getattr__(self, name):
                    f = getattr(self._e, name)

                    def g(*a, **k):
                        r = f(*a, **k)
                        if self.first is None:
                            self.first = r
                        return r
                    return g

            def run(engname, e):
                for waits, fn, sk, inc in self.streams[engname]:
                    if fn is None:
                        for wsk, wval in waits:
                            e.wait_ge(sems[wsk], wval)
                        continue
                    for wsk, wval in waits[:-1]:
                        e.wait_ge(sems[wsk], wval)
                    rec = Rec(e)
                    ins = fn(rec)
                    if waits:
                        wsk, wval = waits[-1]
                        rec.first._wait_ge(sems[wsk], wval)
                    ins.then_inc(sems[sk], inc)

            @block.tensor
            def _(e):
                run("pe", e)

            @block.scalar
            def _(e):
                run("act", e)

            @block.vector
            def _(e):
                run("dve", e)

            @block.gpsimd
            def _(e):
                run("pool", e)

            @block.sync
            def _(e):
                run("sp", e)


def build_program(debug=False):
    nc = bass.Bass("TRN2", target_bir_lowering=False)
    D = lambda name, shape, dt, kind: nc.dram_tensor(name, shape, dt, kind=kind)
    xin_t = D("xin", [NTOK, 1024], F32, "ExternalInput")
    w_in_t = D("w_in", [22 * 128, 1024], F32, "ExternalInput")
    w_out_t = D("w_out", [8 * 128, 1024], F32, "ExternalInput")
    nw_t = D("nw", [128, 8], F32, "ExternalInput")
    qkw_t = D("qkw", [128, 2], F32, "ExternalInput")
    sinks_t = D("sinks_t", [128, 4], F32, "ExternalInput")
    dww_t = D("dww", [128, 4 * 31], F32, "ExternalInput")
    cvec_t = D("cvec", [128, 12], F32, "ExternalInput")
    relb_t = D("relb", [128, 8], F32, "ExternalInput")
    oh_t = D("oh", [128, 512], F32, "ExternalInput")
    ident_t = D("identf", [128, 128], F32, "ExternalInput")
    pm_t = D("pm", [128, 1], F32, "ExternalInput")
    y_t = D("y", [SEQ_PER_CORE, 1024], F32, "ExternalOutput")
    scr_t = D("scr", [128, 4096], F32, "Internal")
    xin, w_in, w_out, yout, scr = xin_t.ap(), w_in_t.ap(), w_out_t.ap(), y_t.ap(), scr_t.ap()

    P = Plan()
    dbg = {}
    with ExitStack() as st:
        def T(name, shape, dt):
            return st.enter_context(nc.sbuf_tensor(name, shape, dt))

        Wp = T("Wp", [128, 8, 2816], BF16)
        Wo = T("Wo", [128, 8, 1024], BF16)
        diag = T("diag", [128, 4 * 31, 128], BF16)
        wst = [T("wst%d" % i, [128, 1024], F32) for i in range(2)]
        xs = [T("xs%d" % i, [128, 1024], F32) for i in range(2)]
        hb = [T("hb%d" % i, [128, 1024], BF16) for i in range(4)]
        hT = T("hT", [128, 8, 512], BF16)
        hTh = T("hTh", [128, 8, 128], BF16)
        cur = {"hT": hT, "hk": "hT"}
        kT = T("kT", [128, NTOK], BF16)
        Vp0 = T("Vp0", [128, 5, 128], BF16)
        Vp1 = T("Vp1", [128, 5, 128], BF16)
        uT = T("uT", [128, 4, 544], BF16)
        qT = T("qT", [128, 4, 512], BF16)
        zaT = T("zaT", [128, 4, 512], BF16)
        zcT = T("zcT", [128, 4, 512], BF16)
        yT = T("yT", [128, 8, 512], BF16)
        ebias = T("ebias", [128, 4, 512], BF16)
        bias = yT[:].rearrange("p a b -> p (a b)").bitcast(F32).rearrange("p (a b) -> p a b", a=4)
        YTK = ["yTa0", "yTa1", "yTa2", "yTa3", "yTc0", "yTc1", "yTc2", "yTc3"]
        PT = [T("PT%d" % i, [128, 2, 512], BF16) for i in range(4)]
        NTMP = 9
        tmp = [T("tmp%d" % i, [128, 512], F32) for i in range(NTMP)]
        sqb = [T("sqb%d" % i, [128, 512], BF16) for i in range(2)]
        ln_mean = T("ln_mean", [128, 512], F32)
        ln_rstd = T("ln_rstd", [128, 512], F32)
        ycb = T("ycb", [128, 4, 512], BF16)
        ysq = T("ysq", [128, 4, 512], BF16)
        nw = T("nw_s", [128, 8], F32)
        qkw = T("qkw_s", [128, 2], F32)
        wq8 = T("wq8", [128, 1], F32)
        sink_s = T("sink_s", [128, 4], F32)
        esink = T("esink", [128, 4], F32)
        dww = T("dww_s", [128, 4 * 31], F32)
        cvec = T("cvec_s", [128, 12], F32)
        ncv = T("ncv", [128, 12], F32)
        relb = T("relb_s", [128, 8], F32)
        relb_hb = T("relb_hb", [128, 8], BF16)
        sel = T("sel", [128, 8], F32)
        oh = T("oh_s", [128, 512], F32)
        ones_b = T("ones_b", [128, 128], BF16)
        identf = T("identf_s", [128, 128], F32)
        identb = T("identb", [128, 128], BF16)
        pm = T("pm_s", [128, 1], F32)
        onesblk = T("onesblk", [128, 128], BF16)
        ones512 = T("ones512", [128, 128], BF16)
        od0 = T("od0", [128, 128], BF16)
        od1 = T("od1", [128, 128], BF16)
        ss = T("ss", [128, 20], F32)
        rsl = T("rsl", [128, 20], F32)
        rs = T("rs", [128, 20], F32)
        epsq = T("epsq", [128, 1], F32)
        epsl = T("epsl", [128, 1], F32)
        epsx = T("epsx", [128, 1], F32)
        onec = T("onec", [128, 1], F32)

        banks = [st.enter_context(nc.psum_tensor("ps%d" % i, [128, 512], F32)) for i in range(8)]
        bank_ctr = [0]

        def new_bank():
            i = bank_ctr[0] % 8
            bank_ctr[0] += 1
            return banks[i], "ps%d" % i

        tmp_ctr = [0]

        def new_tmp():
            i = tmp_ctr[0] % NTMP
            tmp_ctr[0] += 1
            return tmp[i], "tmp%d" % i

        sq_ctr = [0]

        def new_sq():
            i = sq_ctr[0] % 2
            sq_ctr[0] += 1
            return sqb[i], "sqb%d" % i

        def dma(eng, out, in_, reads, writes, key):
            P.op(eng, lambda e: e.dma_start(out=out, in_=in_), reads=reads, writes=writes, dma_key=key)

        def act(out, in_, func, reads, writes, bias=None, scale=None, accum_out=None):
            kw = {}
            if bias is not None:
                kw["bias"] = bias
            if scale is not None:
                kw["scale"] = scale
            if accum_out is not None:
                kw["accum_out"] = accum_out
            P.op("act", lambda e: e.activation(out=out, in_=in_, func=func, **kw), reads=reads, writes=writes)

        def sigmoid_chain(src, skey, extra_reads=(), scale=-1.0, bias=None, n=512):
            t, tk = new_tmp()
            act(t[:, 0:n], src, ACTF.Exp, [skey] + list(extra_reads), [tk], scale=scale, bias=bias)
            act(t[:, 0:n], t[:, 0:n], ACTF.Ln, [tk], [tk], bias=onec[:, 0:1])
            act(t[:, 0:n], t[:, 0:n], ACTF.Exp, [tk], [tk], scale=-1.0)
            return t, tk

        def rstd_chain(src, skey, epsap, n=512):
            t, tk = new_tmp()
            act(t[:, 0:n], src, ACTF.Ln, [skey], [tk], bias=epsap)
            act(t[:, 0:n], t[:, 0:n], ACTF.Exp, [tk], [tk], scale=-0.5)
            return t, tk

        for (dst, src, k) in ((nw, nw_t, "nw"), (identf, ident_t, "identf"), (qkw, qkw_t, "qkw")):
            dma("sp", dst[:], src.ap(), [], [k], k)

        def late_consts():
            for (dst, src, k) in ((sink_s, sinks_t, "sink_s"), (dww, dww_t, "dww"), (cvec, cvec_t, "cvec"), (relb, relb_t, "relb"),
                                  (oh, oh_t, "oh"), (pm, pm_t, "pm")):
                dma("sp", dst[:], src.ap(), [], [k], k)

        def late_const_ops():
            P.op("dve", lambda e: e.tensor_scalar(out=ncv[:], in0=cvec[:], scalar1=-1.0, scalar2=None, op0=ALU.mult),
                 reads=["cvec"], writes=["ncv"])
            act(esink[:], sink_s[:], ACTF.Exp, ["sink_s"], ["esink"])

        def ms(t, val, key, ap=None):
            a = t[:] if ap is None else ap
            P.op("pool", lambda e: e.memset(a, val), writes=[key])

        ms(ones_b, 1.0, "ones_b")
        ms(onesblk, 0.0, "onesblk")
        ms(onesblk, 1.0 / 64, "onesblk", onesblk[0:64, 0:64])
        ms(onesblk, 1.0 / 64, "onesblk", onesblk[64:128, 64:128])
        ms(ones512, 1.0 / 512, "ones512")
        ms(od0, 0.0, "od0")
        ms(od0, 1.0, "od0", od0[:, 0:64])
        ms(od1, 0.0, "od1")
        ms(od1, 1.0, "od1", od1[:, 64:128])
        ms(Vp0, 0.0, "Vp0")
        ms(Vp1, 0.0, "Vp1")
        ms(epsq, 1e-6, "epsq")
        ms(epsl, 1e-5, "epsl")
        ms(epsx, 1e-6, "epsx")
        ms(onec, 1.0, "onec")
        ms(ss, 0.0, "ss")
        P.op("dve", lambda e: e.tensor_copy(out=identb[:], in_=identf[:]), reads=["identf"], writes=["identb"])
        P.op("dve", lambda e: e.tensor_scalar(out=wq8[:], in0=qkw[:, 0:1], scalar1=0.125, scalar2=None, op0=ALU.mult),
             reads=["qkw"], writes=["wq8"])

        xs_ctr = [0]

        hb_ctr = [0]

        def x_dma(t0, i):
            r = xs_ctr[0] % 2
            xs_ctr[0] += 1
            dma("sp", xs[r][:], xin[t0 + i * 128:t0 + (i + 1) * 128, :], [], ["xs%d" % r], "xs%d" % r)
            return r

        def x_chain(t0, i, r):
            rh = hb_ctr[0] % 4
            hb_ctr[0] += 1
            idx = (t0 // 128) + i
            xk, hk = "xs%d" % r, "hb%d" % rh
            act(hb[rh][:], xs[r][:], ACTF.Square, [xk], [hk, "ss"], accum_out=ss[:, idx:idx + 1])
            act(rsl[:, idx:idx + 1], ss[:, idx:idx + 1], ACTF.Ln, ["ss"], ["rsl"], scale=1.0 / 1024, bias=epsx[:, 0:1])
            act(rs[:, idx:idx + 1], rsl[:, idx:idx + 1], ACTF.Exp, ["rsl"], ["rs"], scale=-0.5)
            P.op("dve", lambda e, r=r, rh=rh, idx=idx: e.tensor_scalar(out=hb[rh][:], in0=xs[r][:], scalar1=rs[:, idx:idx + 1],
                                                                       scalar2=None, op0=ALU.mult),
                 reads=[xk, "rs"], writes=[hk])
            return rh

        def loadA(t0, ntok):
            hs = []
            for i in range(ntok // 128):
                hs.append(x_chain(t0, i, x_dma(t0, i)))
            return hs

        def loadA_early(t0):
            return [x_dma(t0, 0), x_dma(t0, 1)]

        def loadA_late(t0, rs01):
            hs = [x_chain(t0, 0, rs01[0]), x_chain(t0, 1, rs01[1])]
            r2 = x_dma(t0, 2)
            r3 = x_dma(t0, 3)
            hs.append(x_chain(t0, 2, r2))
            hs.append(x_chain(t0, 3, r3))
            return hs

        def loadB(hs, dst=None, dkey="hT"):
            dst = hT if dst is None else dst
            for i, rh in enumerate(hs):
                hk = "hb%d" % rh
                bk, bkey = new_bank()
                bkb = bk[:].bitcast(BF16)

                def tr(e, rh=rh, bkb=bkb):
                    for kc in range(8):
                        ins = e.transpose(out=bkb[:, kc * 128:(kc + 1) * 128], in_=hb[rh][:, kc * 128:(kc + 1) * 128],
                                          identity=identb[:])
                    return ins
                P.op("pe", tr, reads=[hk, "identb"], writes=[bkey])
                P.op("dve", lambda e, i=i, bkb=bkb: e.tensor_copy(out=dst[:, :, i * 128:(i + 1) * 128],
                                                                  in_=bkb.rearrange("p (k t) -> p k t", k=8)),
                     reads=[bkey], writes=[dkey])

        w_in_r = w_in.rearrange("(k p) c -> p k c", p=128)
        wst_ctr = [0]
        wo_ctr = [0]

        def f32view(t):
            return t[:].rearrange("p a b -> p (a b)").bitcast(F32)
        wslots = [(wst[0][:], ["wst0"]), (wst[1][:], ["wst1"]), (f32view(zaT), ["zaT"]), (f32view(zcT), ["zcT"]),
                  (f32view(qT), ["qT"]), (f32view(ycb), ["ycb%d" % c for c in range(4)]),
                  (f32view(ysq), ["ysq%d" % c for c in range(4)])]

        def load_w_in_block(blk, permute):
            r = wst_ctr[0] % len(wslots)
            wst_ctr[0] += 1
            stage, wkeys_ = wslots[r]
            st3 = stage.rearrange("p (k c) -> p k c", k=8)
            dma("sp", stage, w_in[blk * 128:(blk + 1) * 128, :], [], wkeys_, "wslot%d" % r)
            out = Wp[:, :, blk * 128:(blk + 1) * 128]
            nwb = nw[:].rearrange("p (k o) -> p k o", o=1).broadcast_to([128, 8, 128])
            P.op("dve", lambda e: e.tensor_tensor(out=out, in0=st3, in1=nwb, op=ALU.mult),
                 reads=wkeys_ + ["nw"], writes=["Wp%d" % blk])

        def load_w_out_block(blk):
            r = wo_ctr[0] % 2
            wo_ctr[0] += 1
            wk = "wst%d" % r
            st3 = wst[r][:].rearrange("p (k c) -> p k c", k=8)
            cs = slice(blk * 128, (blk + 1) * 128)
            subkeys = [wk, wk + "b", wk + "c"]
            dma("sp", wst[r][:], w_out[blk * 128:(blk + 1) * 128, :], [], subkeys, wk)
            P.op("dve", lambda e: e.tensor_copy(out=Wo[:, :, cs], in_=st3), reads=subkeys, writes=["Wo"])

        COL_K, COL_V, COL_ZA, COL_A, COL_G, COL_ZC = 512, 640, 768, 1280, 1792, 2304

        def wkeys(c0):
            return ["Wp%d" % (c0 // 128)]

        def fm_matmul(c0, n, t_lo=0):
            bk, bkey = new_bank()

            src, skey = cur["hT"], cur["hk"]

            def f(e):
                for kc in range(8):
                    ins = e.matmul(bk[:, 0:n - t_lo], lhsT=Wp[:, kc, c0:c0 + 128], rhs=src[:, kc, t_lo:n], start=(kc == 0), stop=(kc == 7))
                return ins
            P.op("pe", f, reads=[skey] + wkeys(c0), writes=[bkey])
            return bk, bkey

        def qk_stage1(c0, n, wcol, out_ap, out_key):
            bk, bkey = fm_matmul(c0, n)
            sq, sqk = new_sq()
            act(sq[:, 0:n], bk[:, 0:n], ACTF.Square, [bkey], [sqk])
            return (bk, bkey, sq, sqk, n, wcol, out_ap, out_key)

        def qk_stage2(stt):
            bk, bkey, sq, sqk, n, wcol, out_ap, out_key = stt
            b2, b2k = new_bank()
            P.op("pe", lambda e: e.matmul(b2[:, 0:n], lhsT=onesblk[:], rhs=sq[:, 0:n], start=True, stop=True),
                 reads=[sqk, "onesblk"], writes=[b2k])
            rt, rtk = rstd_chain(b2[:, 0:n], b2k, epsq[:, 0:1], n)
            P.op("dve", lambda e: e.scalar_tensor_tensor(out=out_ap, in0=bk[:, 0:n], scalar=wcol, in1=rt[:, 0:n],
                                                         op0=ALU.mult, op1=ALU.mult),
                 reads=[bkey, rtk, "qkw", "wq8"], writes=[out_key])

        def proj(t0, n, full, tails=(), tails_per=1, q_first=False):
            tails = list(tails)
            qpend = [None]

            def q_chunks():
                pend = None
                for hh in range(4):
                    stt = qk_stage1(hh * 128, n, wq8[:, 0:1], qT[:, hh, 0:n], "qT")
                    if pend is not None:
                        qk_stage2(pend)
                    pend = stt
                qpend[0] = pend

            def q_flush():
                if qpend[0] is not None:
                    qk_stage2(qpend[0])
                    qpend[0] = None

            def pop_tail(k=1):
                for _ in range(k):
                    if tails:
                        tails.pop(0)()
            tile0 = 1 if full else 0
            kst = qk_stage1(COL_K, n, qkw[:, 1:2], kT[:, t0:t0 + n], "kT")
            bk, bkey = new_bank()
            nt = n // 128

            vsrc, vkey = cur["hT"], cur["hk"]

            def fv(e):
                for i in range(nt):
                    for kc in range(8):
                        ins = e.matmul(bk[:, i * 128:(i + 1) * 128], lhsT=vsrc[:, kc, i * 128:(i + 1) * 128],
                                       rhs=Wp[:, kc, COL_V:COL_V + 128], start=(kc == 0), stop=(kc == 7))
                return ins
            P.op("pe", fv, reads=[vkey] + wkeys(COL_V), writes=[bkey])
            qk_stage2(kst)
            bk3 = bk[:, 0:n].rearrange("p (i c) -> p i c", c=128)
            P.op("dve", lambda e: e.tensor_copy(out=Vp0[:, tile0:tile0 + nt, 0:64], in_=bk3[:, :, 0:64]), reads=[bkey], writes=["Vp0"])
            P.op("dve", lambda e: e.tensor_copy(out=Vp1[:, tile0:tile0 + nt, 64:128], in_=bk3[:, :, 64:128]), reads=[bkey],
                 writes=["Vp1"])
            if full and q_first:
                q_chunks()
            for c in range(4):
                if not full:
                    ba, bak = fm_matmul(COL_A + c * 128, n, t_lo=n - 32)
                    bg, bgk = fm_matmul(COL_G + c * 128, n, t_lo=n - 32)
                    s, sk = sigmoid_chain(bg[:, 0:32], bgk, n=32)
                    P.op("dve", lambda e, c=c, ba=ba, s=s: e.tensor_tensor(out=uT[:, c, 0:32], in0=ba[:, 0:32], in1=s[:, 0:32], op=ALU.mult),
                         reads=[bak, sk], writes=["uT"])
                    pop_tail(tails_per)
                    continue
                ba, bak = fm_matmul(COL_A + c * 128, n)
                q_flush()
                bg, bgk = fm_matmul(COL_G + c * 128, n)
                s, sk = sigmoid_chain(bg[:, 0:n], bgk, n=n)
                if full:
                    P.op("dve", lambda e, c=c, ba=ba, s=s: e.tensor_tensor(out=uT[:, c, 32:32 + n], in0=ba[:, 0:n], in1=s[:, 0:n],
                                                                          op=ALU.mult),
                         reads=[bak, sk], writes=["uT"])
                else:
                    P.op("dve", lambda e, c=c, ba=ba, s=s: e.tensor_tensor(out=uT[:, c, 0:32], in0=ba[:, n - 32:n],
                                                                          in1=s[:, n - 32:n], op=ALU.mult),
                         reads=[bak, sk], writes=["uT"])
                pop_tail(tails_per)
            while tails:
                pop_tail()
            if full:
                def gate_chunk(col, dst, dk, c):
                    bz, bzk = fm_matmul(col + c * 128, n)
                    s, sk = sigmoid_chain(bz[:, 0:n], bzk, n=n)
                    P.op("dve", lambda e: e.tensor_tensor(out=dst[:, c, 0:n], in0=bz[:, 0:n], in1=s[:, 0:n], op=ALU.mult),
                         reads=[bzk, sk], writes=[dk])
                if not q_first:
                    q_chunks()
                gate_chunk(COL_ZA, zaT, "zaT", 0)
                q_flush()
                for c in range(1, 4):
                    gate_chunk(COL_ZA, zaT, "zaT", c)
                for c in range(4):
                    gate_chunk(COL_ZC, zcT, "zcT", c)

        pt_ctr = [0]
        esink_b = esink[:].rearrange("p (h o) -> p h o", o=1).broadcast_to([128, 4, 128])

        def attn_qk(t0, b):
            own = t0 + b * 128
            first = (own == HALO)
            rr = []
            for kvg in range(2):
                rr.append(pt_ctr[0] % 4)
                pt_ctr[0] += 1
            for part in range(2):
                k0 = own - 128 + part * 128
                bks = [new_bank(), new_bank()]

                def fqk(e, bks=bks, k0=k0):
                    for hh in range(4):
                        for kvg in range(2):
                            ps = slice(kvg * 64, (kvg + 1) * 64)
                            ins = e.matmul(bks[kvg][0][:, hh * 128:(hh + 1) * 128], lhsT=kT[ps, k0:k0 + 128],
                                           rhs=qT[ps, hh, b * 128:(b + 1) * 128], start=True, stop=True)
                    return ins
                P.op("pe", fqk, reads=["kT", "qT"], writes=[bks[0][1], bks[1][1]])
                for kvg in range(2):
                    r = rr[kvg]
                    ptk = "PT%d" % r
                    bk, bkey = bks[kvg]
                    if first and part == 0:
                        act(PT[r][:, part, :], bk[:], ACTF.Exp, [bkey, "pm"], [ptk], bias=pm[:, 0:1])
                    else:
                        act(PT[r][:, part, :], bk[:], ACTF.Exp, [bkey], [ptk])
                    P.op("dve", lambda e, r=r, kvg=kvg, part=part: e.tensor_tensor(
                        out=PT[r][:, part, :], in0=PT[r][:, part, :], in1=ebias[:, kvg * 2 + part, :], op=ALU.mult),
                        reads=[ptk, "ebias"], writes=[ptk])
            return [(PT[rr[0]], "PT%d" % rr[0]), (PT[rr[1]], "PT%d" % rr[1])]

        def attn_pv(t0, b, pts):
            tile_own = 1 + b
            bn, bnk = new_bank()
            bd, bdk = new_bank()

            def fpv(e):
                i = 0
                for kvg in range(2):
                    Vp = Vp0 if kvg == 0 else Vp1
                    for part in range(2):
                        ins = e.matmul(bn[:], lhsT=Vp[:, tile_own - 1 + part, :], rhs=pts[kvg][0][:, part, :],
                                       start=(i == 0), stop=(i == 3))
                        i += 1
                return ins
            P.op("pe", fpv, reads=["Vp0", "Vp1", pts[0][1], pts[1][1]], writes=[bnk])

            def fden(e):
                i = 0
                for kvg in range(2):
                    od = od0 if kvg == 0 else od1
                    for part in range(2):
                        ins = e.matmul(bd[:], lhsT=od[:], rhs=pts[kvg][0][:, part, :], start=(i == 0), stop=(i == 3))
                        i += 1
                return ins
            P.op("pe", fden, reads=["od0", "od1", pts[0][1], pts[1][1]], writes=[bdk])
            dt_, dtk = new_tmp()
            dt3 = dt_[:].rearrange("p (h q) -> p h q", h=4)
            P.op("dve", lambda e: e.tensor_tensor(out=dt3, in0=bd[:].rearrange("p (h q) -> p h q", h=4), in1=esink_b, op=ALU.add),
                 reads=[bdk, "esink"], writes=[dtk])
            act(dt_[:], dt_[:], ACTF.Ln, [dtk], [dtk])
            act(dt_[:], dt_[:], ACTF.Exp, [dtk], [dtk], scale=-1.0)
            P.op("dve", lambda e: e.tensor_tensor(out=dt3, in0=dt3, in1=zaT[:, :, b * 128:(b + 1) * 128], op=ALU.mult),
                 reads=[dtk, "zaT"], writes=[dtk])
            P.op("dve", lambda e: e.tensor_tensor(out=yT[:, 0:4, b * 128:(b + 1) * 128], in0=bn[:].rearrange("p (h q) -> p h q", h=4),
                                                  in1=dt3, op=ALU.mult),
                 reads=[bnk, dtk], writes=["yTa%d" % b])

        def attn(t0, outproj_of=None, extra=()):
            prev = None
            extra = list(extra)
            slots = list(outproj_of[1]) if outproj_of is not None else None
            for b in range(4):
                pts = attn_qk(t0, b)
                if outproj_of is not None:
                    outproj_tile(outproj_of[0], b, slots)
                if extra:
                    extra.pop(0)()
                if prev is not None:
                    attn_pv(t0, prev[0], prev[1])
                prev = (b, pts)
            attn_pv(t0, prev[0], prev[1])

        def conv_chunk(c):
            if True:
                bk, bkey = new_bank()

                def fc(e, c=c, bk=bk):
                    for j in range(31):
                        ins = e.matmul(bk[:], lhsT=diag[:, c * 31 + j, :], rhs=uT[:, c, 2 + j:2 + j + 512],
                                       start=(j == 0), stop=(j == 30))
                    return ins
                P.op("pe", fc, reads=["uT", "diag%d" % c], writes=[bkey])
                act(ycb[:, c, :], bk[:], ACTF.Identity, [bkey, "cvec"], ["ycb%d" % c], bias=cvec[:, c:c + 1])
                act(ysq[:, c, :], bk[:], ACTF.Square, [bkey, "cvec"], ["ysq%d" % c], bias=cvec[:, c:c + 1])

        def conv(t0, chunks_done=False, mid=None):
            if not chunks_done:
                for c in range(4):
                    conv_chunk(c)
                    if c == 1 and mid is not None:
                        mid()
            elif mid is not None:
                mid()
            P.op("pool", lambda e: e.tensor_copy(out=uT[:, :, 0:32], in_=uT[:, :, 512:544]), reads=["uT"], writes=["uT"])
            bm, bmk = new_bank()
            be, bek = new_bank()

            def fm(e):
                for c in range(4):
                    ins = e.matmul(bm[:], lhsT=ones512[:], rhs=ycb[:, c, :], start=(c == 0), stop=(c == 3))
                return ins
            P.op("pe", fm, reads=["ones512"] + ["ycb%d" % c for c in range(4)], writes=[bmk])

            def fe(e):
                for c in range(4):
                    ins = e.matmul(be[:], lhsT=ones512[:], rhs=ysq[:, c, :], start=(c == 0), stop=(c == 3))
                return ins
            P.op("pe", fe, reads=["ones512"] + ["ysq%d" % c for c in range(4)], writes=[bek])
            mean, mk = ln_mean, "ln_mean"
            act(mean[:], bm[:], ACTF.Copy, [bmk], [mk])
            var, vk = new_tmp()
            P.op("dve", lambda e: e.tensor_tensor(out=var[:], in0=mean[:], in1=mean[:], op=ALU.mult), reads=[mk], writes=[vk])
            P.op("dve", lambda e: e.tensor_tensor(out=var[:], in0=be[:], in1=var[:], op=ALU.subtract), reads=[bek, vk], writes=[vk])
            rstd, rk = ln_rstd, "ln_rstd"
            act(rstd[:], var[:], ACTF.Ln, [vk], [rk], bias=epsl[:, 0:1])
            act(rstd[:], rstd[:], ACTF.Exp, [rk], [rk], scale=-0.5)

            def tail(c):
                n_, nk = new_tmp()
                P.op("pool", lambda e, c=c, n_=n_: e.tensor_tensor(out=n_[:], in0=ycb[:, c, :], in1=mean[:], op=ALU.subtract),
                     reads=["ycb%d" % c, mk], writes=[nk])
                P.op("pool", lambda e, n_=n_: e.tensor_tensor(out=n_[:], in0=n_[:], in1=rstd[:], op=ALU.mult),
                     reads=[nk, rk], writes=[nk])
                s, sk = sigmoid_chain(n_[:], nk, extra_reads=["ncv"], scale=ncv[:, 4 + c:5 + c], bias=ncv[:, 8 + c:9 + c])
                P.op("dve", lambda e, c=c, n_=n_: e.tensor_scalar(out=n_[:], in0=n_[:], scalar1=cvec[:, 4 + c:5 + c],
                                                                  scalar2=cvec[:, 8 + c:9 + c], op0=ALU.mult, op1=ALU.add),
                     reads=[nk, "cvec", sk], writes=[nk])
                P.op("dve", lambda e, n_=n_, s=s: e.tensor_tensor(out=n_[:], in0=n_[:], in1=s[:], op=ALU.mult),
                     reads=[nk, sk], writes=[nk])
                P.op("dve", lambda e, c=c, n_=n_: e.tensor_tensor(out=yT[:, 4 + c, :], in0=n_[:], in1=zcT[:, c, :], op=ALU.mult),
                     reads=[nk, "zcT"], writes=["yTc%d" % c])
            return [lambda c=c: tail(c) for c in range(4)]

        xr_ctr = [0]

        def xres_load(t0, i):
            r = xr_ctr[0] % 2
            xr_ctr[0] += 1
            xrk = "wst%d" % r
            row = t0 + i * 128
            dma("sp", wst[r][:], xin[row:row + 128, :], [], [xrk, xrk + "b", xrk + "c"], xrk)
            return r

        def outproj_pre(t0):
            return [xres_load(t0, 0), xres_load(t0, 1)]

        def outproj_finish(t0, i, slots, bks, last=False):
            r = slots[i]
            if last and i >= 2:
                xr = xs[i - 2][:]
                xrk = "xs%d" % (i - 2)
            else:
                xr = wst[r][:]
                xrk = "wst%d" % r
            for half, (bk, bkey) in enumerate(bks):
                P.op("dve", lambda e, bk=bk, half=half, xr=xr: e.tensor_tensor(
                    out=xr[:, half * 512:(half + 1) * 512], in0=bk[:], in1=xr[:, half * 512:(half + 1) * 512], op=ALU.add),
                    reads=[bkey, xrk], writes=[xrk])
            orow = t0 + i * 128 - HALO
            if last:
                dma("pool" if i % 2 == 0 else "sp", yout[orow:orow + 128, :], xr, [xrk], [], "yl%d" % i)
            else:
                dma("pool", yout[orow:orow + 128, :], xr, [xrk], [], "yo%d" % r)
                if i + 2 < 4:
                    slots.append(xres_load(t0, i + 2))

        def outproj_tile(t0, i, slots):
            bks = []
            for half in range(2):
                bk, bkey = new_bank()

                def fo(e, bk=bk, half=half):
                    for kc in range(8):
                        ins = e.matmul(bk[:], lhsT=yT[:, kc, i * 128:(i + 1) * 128], rhs=Wo[:, kc, half * 512:(half + 1) * 512],
                                       start=(kc == 0), stop=(kc == 7))
                    return ins
                P.op("pe", fo, reads=["yTa%d" % i, "Wo"] + ["yTc%d" % c for c in range(4)], writes=[bkey])
                bks.append((bk, bkey))
            outproj_finish(t0, i, slots, bks)

        def outproj_last(t0, tails):
            slots = outproj_pre(t0) + [2, 3]
            for i in (2, 3):
                row = t0 + i * 128
                dma("sp", xs[i - 2][:], xin[row:row + 128, :], [], ["xs%d" % (i - 2)], "xs%d" % (i - 2))
            allb = []
            for i in range(4):
                bks = []
                for half in range(2):
                    bk, bkey = new_bank()

                    def fo(e, bk=bk, half=half, i=i):
                        for kc in range(4):
                            ins = e.matmul(bk[:], lhsT=yT[:, kc, i * 128:(i + 1) * 128], rhs=Wo[:, kc, half * 512:(half + 1) * 512],
                                           start=(kc == 0), stop=False)
                        return ins
                    P.op("pe", fo, reads=["yTa%d" % i, "Wo"], writes=[bkey])
                    bks.append((bk, bkey))
                allb.append(bks)
            for c in range(4):
                tails[c]()

                def fo2(e, c=c):
                    for i in range(4):
                        for half, (bk, bkey) in enumerate(allb[i]):
                            ins = e.matmul(bk[:], lhsT=yT[:, 4 + c, i * 128:(i + 1) * 128], rhs=Wo[:, 4 + c, half * 512:(half + 1) * 512],
                                           start=False, stop=(c == 3))
                    return ins
                P.op("pe", fo2, reads=["yTc%d" % c, "Wo"], writes=[bkey for bks in allb for (bk, bkey) in bks])
            for i in range(4):
                outproj_finish(t0, i, slots, allb[i], last=True)

        def build_bias():
            P.op("dve", lambda e: e.tensor_copy(out=relb_hb[:], in_=relb[:]), reads=["relb"], writes=["relb_hb"])
            P.op("dve", lambda e: e.tensor_copy(out=sel[:], in_=relb_hb[:]), reads=["relb_hb"], writes=["sel"])
            P.op("dve", lambda e: e.tensor_tensor(out=sel[64:128, :], in0=relb[64:128, :], in1=sel[64:128, :], op=ALU.subtract),
                 reads=["relb", "sel"], writes=["sel"])
            for h in range(8):
                R, Rk = new_sq()
                P.op("dve", lambda e, h=h, R=R: e.tensor_scalar(out=R[:], in0=oh[:], scalar1=sel[:, h:h + 1], scalar2=None,
                                                                op0=ALU.mult), reads=["oh", "sel"], writes=[Rk])
                bk, bkey = new_bank()
                P.op("pe", lambda e, R=R, bk=bk: e.matmul(bk[:], lhsT=ones_b[:], rhs=R[:], start=True, stop=True),
                     reads=[Rk, "ones_b"], writes=[bkey])
                S, Sk = new_tmp()
                act(S[:], bk[:], ACTF.Copy, [bkey], [Sk])
                dma("sp", scr[:, h * 512:(h + 1) * 512], S[:], [Sk], ["scr%d" % h], "scr%d" % h)
            for g in range(2):
                for part in range(2):
                    src = bass.AP(scr_t, (g * 8 + part) * 256 + 127, [[4095, 128], [512, 4], [1, 128]])
                    dst = bias[:, g * 2 + part, :].rearrange("p (h q) -> p h q", h=4)
                    P.op("sp", lambda e, src=src, dst=dst: e.dma_start(out=dst, in_=src), reads=["scr%d" % h for h in range(8)],
                         writes=["bias%d" % (g * 2 + part)], dma_key="bias%d" % (g * 2 + part))
                    act(ebias[:, g * 2 + part, :], bias[:, g * 2 + part, :], ACTF.Exp, ["bias%d" % (g * 2 + part)] + YTK, ["ebias"])

        def build_diag(cs=range(4)):
            for c in cs:
                in0 = identf[:].rearrange("p (o c) -> p o c", o=1).broadcast_to([128, 31, 128])
                in1 = dww[:, c * 31:(c + 1) * 31].rearrange("p (j o) -> p j o", o=1).broadcast_to([128, 31, 128])
                P.op("dve", lambda e, c=c, in0=in0, in1=in1: e.tensor_tensor(out=diag[:, c * 31:(c + 1) * 31, :], in0=in0, in1=in1,
                                                                            op=ALU.mult),
                     reads=["identf", "dww"], writes=["diag%d" % c])

        def roll_v():
            P.op("pool", lambda e: e.tensor_copy(out=Vp0[:, 0, :], in_=Vp0[:, 4, :]), reads=["Vp0"], writes=["Vp0"])
            P.op("pool", lambda e: e.tensor_copy(out=Vp1[:, 0, :], in_=Vp1[:, 4, :]), reads=["Vp1"], writes=["Vp1"])

        groups = [(HALO + 512 * g, 512) for g in range(4)]
        hsA = loadA(0, HALO)
        for blk in (4, 5):
            load_w_in_block(blk, None)
        loadB(hsA, dst=hTh, dkey="hTh")
        early_g1 = loadA_early(groups[0][0])
        hsG = loadA_late(groups[0][0], early_g1)
        for blk in (10, 11, 12, 13, 14, 15, 16, 17):
            load_w_in_block(blk, None)
        loadB(hsG)
        cur["hT"], cur["hk"] = hTh, "hTh"
        proj(0, HALO, full=False)
        cur["hT"], cur["hk"] = hT, "hT"
        for blk in (0, 1, 2, 3, 6, 7, 8, 9, 18, 19, 20, 21):
            load_w_in_block(blk, None)
        late_consts()
        early = loadA_early(groups[1][0])

        def bias_and_diag0():
            build_bias()
            build_diag([0])
        proj(*groups[0], full=True, tails=[bias_and_diag0] + [lambda c=c: build_diag([c]) for c in (1, 2, 3)], q_first=True)
        late_const_ops()
        box = {}

        def g1_extra(c):
            conv_chunk(c)
            if c == 1:
                box["hs"] = loadA_late(groups[1][0], early)
            load_w_out_block(2 * c)
            load_w_out_block(2 * c + 1)
        attn(groups[0][0], extra=[lambda c=c: g1_extra(c) for c in range(4)])
        hs_next = box["hs"]
        early = None
        for gi in range(4):
            if early is not None:
                hs_next = loadA_late(groups[gi + 1][0], early)
                early = None
            tails = conv(groups[gi][0], chunks_done=(gi == 0), mid=(lambda: loadB(hs_next)) if gi + 1 < 4 else None)
            if gi + 1 < 4:
                roll_v()
                proj(*groups[gi + 1], full=True, tails=tails, q_first=True)
                pre = outproj_pre(groups[gi][0])
                if gi + 2 < 4:
                    early = loadA_early(groups[gi + 2][0])
                attn(groups[gi + 1][0], outproj_of=(groups[gi][0], pre))
            else:
                outproj_last(groups[gi][0], tails)

        if debug:
            for name, t, shape, dt in (("d_kT", kT, [128, NTOK], BF16), ("d_qT", qT, [128, 2048], BF16),
                                       ("d_zaT", zaT, [128, 2048], BF16), ("d_uT", uT, [128, 4 * 544], BF16),
                                       ("d_yT", yT, [128, 4096], BF16), ("d_Vp0", Vp0, [128, 5 * 128], BF16),
                                       ("d_rs", rs, [128, 20], F32),
                                       ("d_zcT", zcT, [128, 2048], BF16)):
                dd = nc.dram_tensor(name, shape, dt, kind="ExternalOutput")
                flat = t[:]
                if len(t[:].shape) == 3:
                    flat = t[:].rearrange("p a b -> p (a b)")
                P.op("pool", lambda e, dd=dd, flat=flat: e.dma_start(out=dd.ap(), in_=flat),
                     reads=["kT", "qT", "zaT", "uT"] + YTK + ["Vp0", "rs", "zcT"],
                     dma_key=name)

        P.final_waits("pool")
        print("sbuf bytes remaining", nc.sbuf_bytes_remaining, "sems", len(P.count))
        sems = {}
        for sk in list(P.count.keys()):
            sems[sk] = st.enter_context(nc.semaphore("s_" + "_".join(str(s) for s in sk)))
        P.emit(nc, sems)
    return nc


def _t5_bucket(n):
    n = np.maximum(n, 0)
    nf = np.maximum(n, 1).astype(np.float32)
    large = 16 + (np.log(nf / np.float32(16)) / np.float32(math.log(128 / 16)) * np.float32(16)).astype(np.int32)
    large = np.minimum(large, 31)
    return np.where(n < 16, n, large)


def _onehot_const():
    oh = np.zeros((64, 512), np.float32)
    for part in range(2):
        for j in range(255):
            u = j - 127
            if part == 0:
                valid = u < 0
                dist = u + 128
            else:
                valid = u >= 0
                dist = u
            col = part * 256 + j
            if valid:
                oh[int(_t5_bucket(np.array(dist))), col] = 1.0
            else:
                oh[32, col] = NEG
    return oh


def kernel(x, norm_w, w_in, q_norm_w, k_norm_w, sinks, dw_w, dw_b, ln_w, ln_b, w_out, rel_bias):
    x = np.asarray(x, np.float32)
    f = lambda a: np.ascontiguousarray(np.asarray(a, np.float32))
    norm_w, w_in, q_norm_w, k_norm_w, sinks = f(norm_w)[0], f(w_in)[0], f(q_norm_w)[0], f(k_norm_w)[0], f(sinks)[0]
    dw_w, dw_b, ln_w, ln_b, w_out, rel_bias = f(dw_w)[0], f(dw_b)[0], f(ln_w)[0], f(ln_b)[0], f(w_out)[0], f(rel_bias)

    nw = np.ascontiguousarray(norm_w.reshape(8, 128).T)
    qkw = np.ascontiguousarray(np.stack([np.tile(q_norm_w, 2), np.tile(k_norm_w, 2)], axis=1))
    sinks_t = np.ascontiguousarray(np.repeat(sinks.reshape(2, 1, 4), 64, axis=1).reshape(128, 4))
    dww = np.ascontiguousarray(dw_w.T.reshape(4, 128, 31).transpose(1, 0, 2).reshape(128, 124))
    cvec = np.ascontiguousarray(np.concatenate([v.reshape(4, 128).T for v in (dw_b, ln_w, ln_b)], axis=1))
    relb = np.zeros((128, 8), np.float32)
    relb[:32] = rel_bias
    relb[32] = 1.0
    relb[64:] = relb[:64]
    oh = np.concatenate([_onehot_const(), _onehot_const()], axis=0)
    identf = np.eye(128, dtype=np.float32)

    def perm_cols(base):
        idx = np.arange(512).reshape(2, 4, 64).transpose(1, 0, 2).reshape(-1)
        return base + idx
    cols = np.arange(2816)
    cols[0:512] = perm_cols(0)
    cols[768:1280] = perm_cols(768)
    w_in_p = w_in[:, cols]
    w_in_tiled = np.ascontiguousarray(w_in_p.reshape(8, 128, 22, 128).transpose(2, 1, 0, 3)).reshape(22 * 128, 1024)
    rows = np.arange(1024)
    rows[0:512] = np.arange(512).reshape(2, 4, 64).transpose(1, 0, 2).reshape(-1)
    w_out_p = w_out[rows, :]
    w_out_tiled = np.ascontiguousarray(w_out_p.reshape(8, 128, 8, 128).transpose(2, 1, 0, 3)).reshape(8 * 128, 1024)

    in_maps = []
    for c in range(NCORES):
        b, half = c // 2, c % 2
        xin = np.zeros((NTOK, 1024), np.float32)
        if half == 1:
            xin[:HALO] = x[b, SEQ_PER_CORE - HALO:SEQ_PER_CORE]
        xin[HALO:] = x[b, half * SEQ_PER_CORE:(half + 1) * SEQ_PER_CORE]
        pm = np.full((128, 1), NEG if half == 0 else 0.0, np.float32)
        in_maps.append({"xin": xin, "w_in": w_in_tiled, "w_out": w_out_tiled, "nw": nw, "qkw": qkw, "sinks_t": sinks_t, "dww": dww,
                        "cvec": cvec, "relb": relb, "oh": oh, "identf": identf, "pm": pm})
    nc = build_program(debug=DEBUG)
    res = run_bass_kernel_spmd(nc, in_maps, core_ids=list(range(NCORES)))
    out = np.empty((4, 4096, 1024), np.float32)
    for c in range(NCORES):
        b, half = c // 2, c % 2
        out[b, half * SEQ_PER_CORE:(half + 1) * SEQ_PER_CORE] = res.results[c]["y"]
    if DEBUG:
        kernel.debug_results = res.results
    return out
```

```python
import math
from contextlib import ExitStack

import numpy as np
import ml_dtypes

import concourse.bass as bass
import concourse.mybir as mybir
from concourse.bass_utils import run_bass_kernel_spmd

F32 = mybir.dt.float32
BF16 = mybir.dt.bfloat16
ALU = mybir.AluOpType
ACTF = mybir.ActivationFunctionType

ENGS = ("pe", "act", "dve", "pool", "sp")
NEG = -30000.0
NCORES = 8
SEQ_PER_CORE = 2048
HALO = 128
NTOK = SEQ_PER_CORE + HALO
DEBUG = False


class Plan:
    def __init__(self):
        self.streams = {e: [] for e in ENGS}
        self.count = {}
        self.waited = {e: {} for e in ENGS}
        self.last_w = {}
        self.readers = {}

    def _need(self, eng, dep, waits):
        if dep is None:
            return
        sk, val = dep
        if eng == "pe" and sk == ("eng", "pe"):
            return
        if self.waited[eng].get(sk, 0) >= val:
            return
        self.waited[eng][sk] = val
        for w in waits:
            if w[0] == sk:
                w[1] = max(w[1], val)
                return
        waits.append([sk, val])

    def op(self, eng, fn, reads=(), writes=(), dma_key=None):
        waits = []
        for k in reads:
            self._need(eng, self.last_w.get(k), waits)
        for k in writes:
            self._need(eng, self.last_w.get(k), waits)
            for r in self.readers.get(k, ()):
                self._need(eng, r, waits)
        if dma_key is not None:
            sk = ("dma", dma_key)
            inc = 16
        else:
            sk = ("eng", eng)
            inc = 1
        val = self.count.get(sk, 0) + inc
        self.count[sk] = val
        done = (sk, val)
        for k in reads:
            self.readers.setdefault(k, []).append(done)
        for k in writes:
            self.last_w[k] = done
            self.readers[k] = []
        self.streams[eng].append((waits, fn, sk, inc))
        return done

    def final_waits(self, eng):
        waits = []
        for sk, val in self.count.items():
            self._need(eng, (sk, val), waits)
        self.streams[eng].append((waits, None, None, 0))

    def emit(self, nc, sems):
        with nc.Block() as block:
            class Rec:
                def __init__(self, e):
                    self._e, self.first = e, None

                def __getattr__(self, name):
                    f = getattr(self._e, name)

                    def g(*a, **k):
                        r = f(*a, **k)
                        if self.first is None:
                            self.first = r
                        return r
                    return g

            def run(engname, e):
                for waits, fn, sk, inc in self.streams[engname]:
                    if fn is None:
                        for wsk, wval in waits:
                            e.wait_ge(sems[wsk], wval)
                        continue
                    for wsk, wval in waits[:-1]:
                        e.wait_ge(sems[wsk], wval)
                    rec = Rec(e)
                    ins = fn(rec)
                    if waits:
                        wsk, wval = waits[-1]
                        rec.first._wait_ge(sems[wsk], wval)
                    ins.then_inc(sems[sk], inc)

            @block.tensor
            def _(e):
                run("pe", e)

            @block.scalar
            def _(e):
                run("act", e)

            @block.vector
            def _(e):
                run("dve", e)

            @block.gpsimd
            def _(e):
                run("pool", e)

            @block.sync
            def _(e):
                run("sp", e)


def build_program(debug=False):
    nc = bass.Bass("TRN2", target_bir_lowering=False)
    D = lambda name, shape, dt, kind: nc.dram_tensor(name, shape, dt, kind=kind)
    xin_t = D("xin", [NTOK, 1024], F32, "ExternalInput")
    w_in_t = D("w_in", [22 * 128, 1024], F32, "ExternalInput")
    w_out_t = D("w_out", [8 * 128, 1024], F32, "ExternalInput")
    nw_t = D("nw", [128, 8], F32, "ExternalInput")
    qkw_t = D("qkw", [128, 2], F32, "ExternalInput")
    sinks_t = D("sinks_t", [128, 4], F32, "ExternalInput")
    dww_t = D("dww", [128, 4 * 31], F32, "ExternalInput")
    cvec_t = D("cvec", [128, 12], F32, "ExternalInput")
    relb_t = D("relb", [128, 8], F32, "ExternalInput")
    oh_t = D("oh", [128, 512], F32, "ExternalInput")
    ident_t = D("identf", [128, 128], F32, "ExternalInput")
    pm_t = D("pm", [128, 1], F32, "ExternalInput")
    y_t = D("y", [SEQ_PER_CORE, 1024], F32, "ExternalOutput")
    scr_t = D("scr", [128, 4096], F32, "Internal")
    xin, w_in, w_out, yout, scr = xin_t.ap(), w_in_t.ap(), w_out_t.ap(), y_t.ap(), scr_t.ap()

    P = Plan()
    dbg = {}
    with ExitStack() as st:
        def T(name, shape, dt):
            return st.enter_context(nc.sbuf_tensor(name, shape, dt))

        Wp = T("Wp", [128, 8, 2816], BF16)
        Wo = T("Wo", [128, 8, 1024], BF16)
        diag = T("diag", [128, 4 * 31, 128], BF16)
        wst = [T("wst%d" % i, [128, 1024], F32) for i in range(2)]
        xs = [T("xs%d" % i, [128, 1024], F32) for i in range(2)]
        hb = [T("hb%d" % i, [128, 1024], BF16) for i in range(4)]
        hT = T("hT", [128, 8, 512], BF16)
        hTh = T("hTh", [128, 8, 128], BF16)
        cur = {"hT": hT, "hk": "hT"}
        kT = T("kT", [128, NTOK], BF16)
        Vp0 = T("Vp0", [128, 5, 128], BF16)
        Vp1 = T("Vp1", [128, 5, 128], BF16)
        uT = T("uT", [128, 4, 544], BF16)
        qT = T("qT", [128, 4, 512], BF16)
        zaT = T("zaT", [128, 4, 512], BF16)
        zcT = T("zcT", [128, 4, 512], BF16)
        yT = T("yT", [128, 8, 512], BF16)
        ebias = T("ebias", [128, 4, 512], BF16)
        bias = yT[:].rearrange("p a b -> p (a b)").bitcast(F32).rearrange("p (a b) -> p a b", a=4)
        YTK = ["yTa0", "yTa1", "yTa2", "yTa3", "yTc0", "yTc1", "yTc2", "yTc3"]
        PT = [T("PT%d" % i, [128, 2, 512], BF16) for i in range(4)]
        NTMP = 9
        tmp = [T("tmp%d" % i, [128, 512], F32) for i in range(NTMP)]
        sqb = [T("sqb%d" % i, [128, 512], BF16) for i in range(2)]
        ln_mean = T("ln_mean", [128, 512], F32)
        ln_rstd = T("ln_rstd", [128, 512], F32)
        ycb = T("ycb", [128, 4, 512], BF16)
        ysq = T("ysq", [128, 4, 512], BF16)
        nw = T("nw_s", [128, 8], F32)
        qkw = T("qkw_s", [128, 2], F32)
        wq8 = T("wq8", [128, 1], F32)
        sink_s = T("sink_s", [128, 4], F32)
        esink = T("esink", [128, 4], F32)
        dww = T("dww_s", [128, 4 * 31], F32)
        cvec = T("cvec_s", [128, 12], F32)
        ncv = T("ncv", [128, 12], F32)
        relb = T("relb_s", [128, 8], F32)
        relb_hb = T("relb_hb", [128, 8], BF16)
        sel = T("sel", [128, 8], F32)
        oh = T("oh_s", [128, 512], F32)
        ones_b = T("ones_b", [128, 128], BF16)
        identf = T("identf_s", [128, 128], F32)
        identb = T("identb", [128, 128], BF16)
        pm = T("pm_s", [128, 1], F32)
        onesblk = T("onesblk", [128, 128], BF16)
        ones512 = T("ones512", [128, 128], BF16)
        od0 = T("od0", [128, 128], BF16)
        od1 = T("od1", [128, 128], BF16)
        ss = T("ss", [128, 20], F32)
        rsl = T("rsl", [128, 20], F32)
        rs = T("rs", [128, 20], F32)
        epsq = T("epsq", [128, 1], F32)
        epsl = T("epsl", [128, 1], F32)
        epsx = T("epsx", [128, 1], F32)
        onec = T("onec", [128, 1], F32)

        banks = [st.enter_context(nc.psum_tensor("ps%d" % i, [128, 512], F32)) for i in range(8)]
        bank_ctr = [0]

        def new_bank():
            i = bank_ctr[0] % 8
            bank_ctr[0] += 1
            return banks[i], "ps%d" % i

        tmp_ctr = [0]

        def new_tmp():
            i = tmp_ctr[0] % NTMP
            tmp_ctr[0] += 1
            return tmp[i], "tmp%d" % i

        sq_ctr = [0]

        def new_sq():
            i = sq_ctr[0] % 2
            sq_ctr[0] += 1
            return sqb[i], "sqb%d" % i

        def dma(eng, out, in_, reads, writes, key):
            P.op(eng, lambda e: e.dma_start(out=out, in_=in_), reads=reads, writes=writes, dma_key=key)

        def act(out, in_, func, reads, writes, bias=None, scale=None, accum_out=None):
            kw = {}
            if bias is not None:
                kw["bias"] = bias
            if scale is not None:
                kw["scale"] = scale
            if accum_out is not None:
                kw["accum_out"] = accum_out
            P.op("act", lambda e: e.activation(out=out, in_=in_, func=func, **kw), reads=reads, writes=writes)

        def sigmoid_chain(src, skey, extra_reads=(), scale=-1.0, bias=None, n=512):
            t, tk = new_tmp()
            act(t[:, 0:n], src, ACTF.Exp, [skey] + list(extra_reads), [tk], scale=scale, bias=bias)
            act(t[:, 0:n], t[:, 0:n], ACTF.Ln, [tk], [tk], bias=onec[:, 0:1])
            act(t[:, 0:n], t[:, 0:n], ACTF.Exp, [tk], [tk], scale=-1.0)
            return t, tk

        def rstd_chain(src, skey, epsap, n=512):
            t, tk = new_tmp()
            act(t[:, 0:n], src, ACTF.Ln, [skey], [tk], bias=epsap)
            act(t[:, 0:n], t[:, 0:n], ACTF.Exp, [tk], [tk], scale=-0.5)
            return t, tk

        for (dst, src, k) in ((nw, nw_t, "nw"), (identf, ident_t, "identf"), (qkw, qkw_t, "qkw")):
            dma("sp", dst[:], src.ap(), [], [k], k)

        def late_consts():
            for (dst, src, k) in ((sink_s, sinks_t, "sink_s"), (dww, dww_t, "dww"), (cvec, cvec_t, "cvec"), (relb, relb_t, "relb"),
                                  (oh, oh_t, "oh"), (pm, pm_t, "pm")):
                dma("sp", dst[:], src.ap(), [], [k], k)

        def late_const_ops():
            P.op("dve", lambda e: e.tensor_scalar(out=ncv[:], in0=cvec[:], scalar1=-1.0, scalar2=None, op0=ALU.mult),
                 reads=["cvec"], writes=["ncv"])
            act(esink[:], sink_s[:], ACTF.Exp, ["sink_s"], ["esink"])

        def ms(t, val, key, ap=None):
            a = t[:] if ap is None else ap
            P.op("pool", lambda e: e.memset(a, val), writes=[key])

        ms(ones_b, 1.0, "ones_b")
        ms(onesblk, 0.0, "onesblk")
        ms(onesblk, 1.0 / 64, "onesblk", onesblk[0:64, 0:64])
        ms(onesblk, 1.0 / 64, "onesblk", onesblk[64:128, 64:128])
        ms(ones512, 1.0 / 512, "ones512")
        ms(od0, 0.0, "od0")
        ms(od0, 1.0, "od0", od0[:, 0:64])
        ms(od1, 0.0, "od1")
        ms(od1, 1.0, "od1", od1[:, 64:128])
        ms(Vp0, 0.0, "Vp0")
        ms(Vp1, 0.0, "Vp1")
        ms(epsq, 1e-6, "epsq")
        ms(epsl, 1e-5, "epsl")
        ms(epsx, 1e-6, "epsx")
        ms(onec, 1.0, "onec")
        ms(ss, 0.0, "ss")
        P.op("dve", lambda e: e.tensor_copy(out=identb[:], in_=identf[:]), reads=["identf"], writes=["identb"])
        P.op("dve", lambda e: e.tensor_scalar(out=wq8[:], in0=qkw[:, 0:1], scalar1=0.125, scalar2=None, op0=ALU.mult),
             reads=["qkw"], writes=["wq8"])

        xs_ctr = [0]

        hb_ctr = [0]

        def x_dma(t0, i):
            r = xs_ctr[0] % 2
            xs_ctr[0] += 1
            dma("sp", xs[r][:], xin[t0 + i * 128:t0 + (i + 1) * 128, :], [], ["xs%d" % r], "xs%d" % r)
            return r

        def x_chain(t0, i, r):
            rh = hb_ctr[0] % 4
            hb_ctr[0] += 1
            idx = (t0 // 128) + i
            xk, hk = "xs%d" % r, "hb%d" % rh
            act(hb[rh][:], xs[r][:], ACTF.Square, [xk], [hk, "ss"], accum_out=ss[:, idx:idx + 1])
            act(rsl[:, idx:idx + 1], ss[:, idx:idx + 1], ACTF.Ln, ["ss"], ["rsl"], scale=1.0 / 1024, bias=epsx[:, 0:1])
            act(rs[:, idx:idx + 1], rsl[:, idx:idx + 1], ACTF.Exp, ["rsl"], ["rs"], scale=-0.5)
            P.op("dve", lambda e, r=r, rh=rh, idx=idx: e.tensor_scalar(out=hb[rh][:], in0=xs[r][:], scalar1=rs[:, idx:idx + 1],
                                                                       scalar2=None, op0=ALU.mult),
                 reads=[xk, "rs"], writes=[hk])
            return rh

        def loadA(t0, ntok):
            hs = []
            for i in range(ntok // 128):
                hs.append(x_chain(t0, i, x_dma(t0, i)))
            return hs

        def loadA_early(t0):
            return [x_dma(t0, 0), x_dma(t0, 1)]

        def loadA_late(t0, rs01):
            hs = [x_chain(t0, 0, rs01[0]), x_chain(t0, 1, rs01[1])]
            r2 = x_dma(t0, 2)
            r3 = x_dma(t0, 3)
            hs.append(x_chain(t0, 2, r2))
            hs.append(x_chain(t0, 3, r3))
            return hs

        def loadB(hs, dst=None, dkey="hT"):
            dst = hT if dst is None else dst
            for i, rh in enumerate(hs):
                hk = "hb%d" % rh
                bk, bkey = new_bank()
                bkb = bk[:].bitcast(BF16)

                def tr(e, rh=rh, bkb=bkb):
                    for kc in range(8):
                        ins = e.transpose(out=bkb[:, kc * 128:(kc + 1) * 128], in_=hb[rh][:, kc * 128:(kc + 1) * 128],
                                          identity=identb[:])
                    return ins
                P.op("pe", tr, reads=[hk, "identb"], writes=[bkey])
                P.op("dve", lambda e, i=i, bkb=bkb: e.tensor_copy(out=dst[:, :, i * 128:(i + 1) * 128],
                                                                  in_=bkb.rearrange("p (k t) -> p k t", k=8)),
                     reads=[bkey], writes=[dkey])

        w_in_r = w_in.rearrange("(k p) c -> p k c", p=128)
        wst_ctr = [0]
        wo_ctr = [0]

        def f32view(t):
            return t[:].rearrange("p a b -> p (a b)").bitcast(F32)
        wslots = [(wst[0][:], ["wst0"]), (wst[1][:], ["wst1"]), (f32view(zaT), ["zaT"]), (f32view(zcT), ["zcT"]),
                  (f32view(qT), ["qT"]), (f32view(ycb), ["ycb%d" % c for c in range(4)]),
                  (f32view(ysq), ["ysq%d" % c for c in range(4)])]

        def load_w_in_block(blk, permute):
            r = wst_ctr[0] % len(wslots)
            wst_ctr[0] += 1
            stage, wkeys_ = wslots[r]
            st3 = stage.rearrange("p (k c) -> p k c", k=8)
            dma("sp", stage, w_in[blk * 128:(blk + 1) * 128, :], [], wkeys_, "wslot%d" % r)
            out = Wp[:, :, blk * 128:(blk + 1) * 128]
            nwb = nw[:].rearrange("p (k o) -> p k o", o=1).broadcast_to([128, 8, 128])
            P.op("dve", lambda e: e.tensor_tensor(out=out, in0=st3, in1=nwb, op=ALU.mult),
                 reads=wkeys_ + ["nw"], writes=["Wp%d" % blk])

        def load_w_out_block(blk):
            r = wo_ctr[0] % 2
            wo_ctr[0] += 1
            wk = "wst%d" % r
            st3 = wst[r][:].rearrange("p (k c) -> p k c", k=8)
            cs = slice(blk * 128, (blk + 1) * 128)
            subkeys = [wk, wk + "b", wk + "c"]
            dma("sp", wst[r][:], w_out[blk * 128:(blk + 1) * 128, :], [], subkeys, wk)
            P.op("pool", lambda e: e.tensor_copy(out=Wo[:, :, cs], in_=st3), reads=subkeys, writes=["Wo"])

        COL_K, COL_V, COL_ZA, COL_A, COL_G, COL_ZC = 512, 640, 768, 1280, 1792, 2304

        def wkeys(c0):
            return ["Wp%d" % (c0 // 128)]

        def fm_matmul(c0, n, t_lo=0):
            bk, bkey = new_bank()

            src, skey = cur["hT"], cur["hk"]

            def f(e):
                for kc in range(8):
                    ins = e.matmul(bk[:, 0:n - t_lo], lhsT=Wp[:, kc, c0:c0 + 128], rhs=src[:, kc, t_lo:n], start=(kc == 0), stop=(kc == 7))
                return ins
            P.op("pe", f, reads=[skey] + wkeys(c0), writes=[bkey])
            return bk, bkey

        def qk_stage1(c0, n, wcol, out_ap, out_key):
            bk, bkey = fm_matmul(c0, n)
            sq, sqk = new_sq()
            act(sq[:, 0:n], bk[:, 0:n], ACTF.Square, [bkey], [sqk])
            return (bk, bkey, sq, sqk, n, wcol, out_ap, out_key)

        def qk_stage2(stt):
            bk, bkey, sq, sqk, n, wcol, out_ap, out_key = stt
            b2, b2k = new_bank()
            P.op("pe", lambda e: e.matmul(b2[:, 0:n], lhsT=onesblk[:], rhs=sq[:, 0:n], start=True, stop=True),
                 reads=[sqk, "onesblk"], writes=[b2k])
            rt, rtk = rstd_chain(b2[:, 0:n], b2k, epsq[:, 0:1], n)
            P.op("dve", lambda e: e.scalar_tensor_tensor(out=out_ap, in0=bk[:, 0:n], scalar=wcol, in1=rt[:, 0:n],
                                                         op0=ALU.mult, op1=ALU.mult),
                 reads=[bkey, rtk, "qkw", "wq8"], writes=[out_key])

        def proj(t0, n, full, tails=(), tails_per=1, q_first=False):
            tails = list(tails)
            qpend = [None]

            def q_chunks():
                pend = None
                for hh in range(4):
                    stt = qk_stage1(hh * 128, n, wq8[:, 0:1], qT[:, hh, 0:n], "qT")
                    if pend is not None:
                        qk_stage2(pend)
                    pend = stt
                qpend[0] = pend

            def q_flush():
                if qpend[0] is not None:
                    qk_stage2(qpend[0])
                    qpend[0] = None

            def pop_tail(k=1):
                for _ in range(k):
                    if tails:
                        tails.pop(0)()
            tile0 = 1 if full else 0
            kst = qk_stage1(COL_K, n, qkw[:, 1:2], kT[:, t0:t0 + n], "kT")
            bk, bkey = new_bank()
            nt = n // 128

            vsrc, vkey = cur["hT"], cur["hk"]

            def fv(e):
                for i in range(nt):
                    for kc in range(8):
                        ins = e.matmul(bk[:, i * 128:(i + 1) * 128], lhsT=vsrc[:, kc, i * 128:(i + 1) * 128],
                                       rhs=Wp[:, kc, COL_V:COL_V + 128], start=(kc == 0), stop=(kc == 7))
                return ins
            P.op("pe", fv, reads=[vkey] + wkeys(COL_V), writes=[bkey])
            qk_stage2(kst)
            bk3 = bk[:, 0:n].rearrange("p (i c) -> p i c", c=128)
            P.op("dve", lambda e: e.tensor_copy(out=Vp0[:, tile0:tile0 + nt, 0:64], in_=bk3[:, :, 0:64]), reads=[bkey], writes=["Vp0"])
            P.op("dve", lambda e: e.tensor_copy(out=Vp1[:, tile0:tile0 + nt, 64:128], in_=bk3[:, :, 64:128]), reads=[bkey],
                 writes=["Vp1"])
            if full and q_first:
                q_chunks()
            for c in range(4):
                if not full:
                    ba, bak = fm_matmul(COL_A + c * 128, n, t_lo=n - 32)
                    bg, bgk = fm_matmul(COL_G + c * 128, n, t_lo=n - 32)
                    s, sk = sigmoid_chain(bg[:, 0:32], bgk, n=32)
                    P.op("dve", lambda e, c=c, ba=ba, s=s: e.tensor_tensor(out=uT[:, c, 0:32], in0=ba[:, 0:32], in1=s[:, 0:32], op=ALU.mult),
                         reads=[bak, sk], writes=["uT"])
                    pop_tail(tails_per)
                    continue
                ba, bak = fm_matmul(COL_A + c * 128, n)
                q_flush()
                bg, bgk = fm_matmul(COL_G + c * 128, n)
                s, sk = sigmoid_chain(bg[:, 0:n], bgk, n=n)
                if full:
                    P.op("dve", lambda e, c=c, ba=ba, s=s: e.tensor_tensor(out=uT[:, c, 32:32 + n], in0=ba[:, 0:n], in1=s[:, 0:n],
                                                                          op=ALU.mult),
                         reads=[bak, sk], writes=["uT"])
                else:
                    P.op("dve", lambda e, c=c, ba=ba, s=s: e.tensor_tensor(out=uT[:, c, 0:32], in0=ba[:, n - 32:n],
                                                                          in1=s[:, n - 32:n], op=ALU.mult),
                         reads=[bak, sk], writes=["uT"])
                pop_tail(tails_per)
            while tails:
                pop_tail()
            if full:
                def gate_chunk(col, dst, dk, c):
                    bz, bzk = fm_matmul(col + c * 128, n)
                    s, sk = sigmoid_chain(bz[:, 0:n], bzk, n=n)
                    P.op("dve", lambda e: e.tensor_tensor(out=dst[:, c, 0:n], in0=bz[:, 0:n], in1=s[:, 0:n], op=ALU.mult),
                         reads=[bzk, sk], writes=[dk])
                if not q_first:
                    q_chunks()
                gate_chunk(COL_ZA, zaT, "zaT", 0)
                q_flush()
                for c in range(1, 4):
                    gate_chunk(COL_ZA, zaT, "zaT", c)
                for c in range(4):
                    gate_chunk(COL_ZC, zcT, "zcT", c)

        pt_ctr = [0]
        esink_b = esink[:].rearrange("p (h o) -> p h o", o=1).broadcast_to([128, 4, 128])

        def attn_qk(t0, b):
            own = t0 + b * 128
            first = (own == HALO)
            rr = []
            for kvg in range(2):
                rr.append(pt_ctr[0] % 4)
                pt_ctr[0] += 1
            for part in range(2):
                k0 = own - 128 + part * 128
                bks = [new_bank(), new_bank()]

                def fqk(e, bks=bks, k0=k0):
                    for hh in range(4):
                        for kvg in range(2):
                            ps = slice(kvg * 64, (kvg + 1) * 64)
                            ins = e.matmul(bks[kvg][0][:, hh * 128:(hh + 1) * 128], lhsT=kT[ps, k0:k0 + 128],
                                           rhs=qT[ps, hh, b * 128:(b + 1) * 128], start=True, stop=True)
                    return ins
                P.op("pe", fqk, reads=["kT", "qT"], writes=[bks[0][1], bks[1][1]])
                for kvg in range(2):
                    r = rr[kvg]
                    ptk = "PT%d" % r
                    bk, bkey = bks[kvg]
                    if first and part == 0:
                        act(PT[r][:, part, :], bk[:], ACTF.Exp, [bkey, "pm"], [ptk], bias=pm[:, 0:1])
                    else:
                        act(PT[r][:, part, :], bk[:], ACTF.Exp, [bkey], [ptk])
                    P.op("dve", lambda e, r=r, kvg=kvg, part=part: e.tensor_tensor(
                        out=PT[r][:, part, :], in0=PT[r][:, part, :], in1=ebias[:, kvg * 2 + part, :], op=ALU.mult),
                        reads=[ptk, "ebias"], writes=[ptk])
            return [(PT[rr[0]], "PT%d" % rr[0]), (PT[rr[1]], "PT%d" % rr[1])]

        def attn_pv(t0, b, pts):
            tile_own = 1 + b
            bn, bnk = new_bank()
            bd, bdk = new_bank()

            def fpv(e):
                i = 0
                for kvg in range(2):
                    Vp = Vp0 if kvg == 0 else Vp1
                    for part in range(2):
                        ins = e.matmul(bn[:], lhsT=Vp[:, tile_own - 1 + part, :], rhs=pts[kvg][0][:, part, :],
                                       start=(i == 0), stop=(i == 3))
                        i += 1
                return ins
            P.op("pe", fpv, reads=["Vp0", "Vp1", pts[0][1], pts[1][1]], writes=[bnk])

            def fden(e):
                i = 0
                for kvg in range(2):
                    od = od0 if kvg == 0 else od1
                    for part in range(2):
                        ins = e.matmul(bd[:], lhsT=od[:], rhs=pts[kvg][0][:, part, :], start=(i == 0), stop=(i == 3))
                        i += 1
                return ins
            P.op("pe", fden, reads=["od0", "od1", pts[0][1], pts[1][1]], writes=[bdk])
            dt_, dtk = new_tmp()
            dt3 = dt_[:].rearrange("p (h q) -> p h q", h=4)
            P.op("dve", lambda e: e.tensor_tensor(out=dt3, in0=bd[:].rearrange("p (h q) -> p h q", h=4), in1=esink_b, op=ALU.add),
                 reads=[bdk, "esink"], writes=[dtk])
            act(dt_[:], dt_[:], ACTF.Ln, [dtk], [dtk])
            act(dt_[:], dt_[:], ACTF.Exp, [dtk], [dtk], scale=-1.0)
            P.op("dve", lambda e: e.tensor_tensor(out=dt3, in0=dt3, in1=zaT[:, :, b * 128:(b + 1) * 128], op=ALU.mult),
                 reads=[dtk, "zaT"], writes=[dtk])
            P.op("dve", lambda e: e.tensor_tensor(out=yT[:, 0:4, b * 128:(b + 1) * 128], in0=bn[:].rearrange("p (h q) -> p h q", h=4),
                                                  in1=dt3, op=ALU.mult),
                 reads=[bnk, dtk], writes=["yTa%d" % b])

        def attn(t0, outproj_of=None, extra=()):
            prev = None
            extra = list(extra)
            slots = list(outproj_of[1]) if outproj_of is not None else None
            for b in range(4):
                pts = attn_qk(t0, b)
                if outproj_of is not None:
                    outproj_tile(outproj_of[0], b, slots)
                if extra:
                    extra.pop(0)()
                if prev is not None:
                    attn_pv(t0, prev[0], prev[1])
                prev = (b, pts)
            attn_pv(t0, prev[0], prev[1])

        def conv_chunk(c):
            if True:
                bk, bkey = new_bank()

                def fc(e, c=c, bk=bk):
                    for j in range(31):
                        ins = e.matmul(bk[:], lhsT=diag[:, c * 31 + j, :], rhs=uT[:, c, 2 + j:2 + j + 512],
                                       start=(j == 0), stop=(j == 30))
                    return ins
                P.op("pe", fc, reads=["uT", "diag%d" % c], writes=[bkey])
                act(ycb[:, c, :], bk[:], ACTF.Identity, [bkey, "cvec"], ["ycb%d" % c], bias=cvec[:, c:c + 1])
                act(ysq[:, c, :], bk[:], ACTF.Square, [bkey, "cvec"], ["ysq%d" % c], bias=cvec[:, c:c + 1])

        def conv(t0, chunks_done=False, mid=None):
            if not chunks_done:
                for c in range(4):
                    conv_chunk(c)
                    if c == 1 and mid is not None:
                        mid()
            elif mid is not None:
                mid()
            P.op("pool", lambda e: e.tensor_copy(out=uT[:, :, 0:32], in_=uT[:, :, 512:544]), reads=["uT"], writes=["uT"])
            bm, bmk = new_bank()
            be, bek = new_bank()

            def fm(e):
                for c in range(4):
                    ins = e.matmul(bm[:], lhsT=ones512[:], rhs=ycb[:, c, :], start=(c == 0), stop=(c == 3))
                return ins
            P.op("pe", fm, reads=["ones512"] + ["ycb%d" % c for c in range(4)], writes=[bmk])

            def fe(e):
                for c in range(4):
                    ins = e.matmul(be[:], lhsT=ones512[:], rhs=ysq[:, c, :], start=(c == 0), stop=(c == 3))
                return ins
            P.op("pe", fe, reads=["ones512"] + ["ysq%d" % c for c in range(4)], writes=[bek])
            mean, mk = ln_mean, "ln_mean"
            act(mean[:], bm[:], ACTF.Copy, [bmk], [mk])
            var, vk = new_tmp()
            P.op("dve", lambda e: e.tensor_tensor(out=var[:], in0=mean[:], in1=mean[:], op=ALU.mult), reads=[mk], writes=[vk])
            P.op("dve", lambda e: e.tensor_tensor(out=var[:], in0=be[:], in1=var[:], op=ALU.subtract), reads=[bek, vk], writes=[vk])
            rstd, rk = ln_rstd, "ln_rstd"
            act(rstd[:], var[:], ACTF.Ln, [vk], [rk], bias=epsl[:, 0:1])
            act(rstd[:], rstd[:], ACTF.Exp, [rk], [rk], scale=-0.5)

            def tail(c):
                n_, nk = new_tmp()
                P.op("pool", lambda e, c=c, n_=n_: e.tensor_tensor(out=n_[:], in0=ycb[:, c, :], in1=mean[:], op=ALU.subtract),
                     reads=["ycb%d" % c, mk], writes=[nk])
                P.op("pool", lambda e, n_=n_: e.tensor_tensor(out=n_[:], in0=n_[:], in1=rstd[:], op=ALU.mult),
                     reads=[nk, rk], writes=[nk])
                s, sk = sigmoid_chain(n_[:], nk, extra_reads=["ncv"], scale=ncv[:, 4 + c:5 + c], bias=ncv[:, 8 + c:9 + c])
                P.op("dve", lambda e, c=c, n_=n_: e.tensor_scalar(out=n_[:], in0=n_[:], scalar1=cvec[:, 4 + c:5 + c],
                                                                  scalar2=cvec[:, 8 + c:9 + c], op0=ALU.mult, op1=ALU.add),
                     reads=[nk, "cvec", sk], writes=[nk])
                P.op("dve", lambda e, n_=n_, s=s: e.tensor_tensor(out=n_[:], in0=n_[:], in1=s[:], op=ALU.mult),
                     reads=[nk, sk], writes=[nk])
                P.op("dve", lambda e, c=c, n_=n_: e.tensor_tensor(out=yT[:, 4 + c, :], in0=n_[:], in1=zcT[:, c, :], op=ALU.mult),
                     reads=[nk, "zcT"], writes=["yTc%d" % c])
            return [lambda c=c: tail(c) for c in range(4)]

        xr_ctr = [0]

        def xres_load(t0, i):
            r = xr_ctr[0] % 2
            xr_ctr[0] += 1
            xrk = "wst%d" % r
            row = t0 + i * 128
            dma("sp", wst[r][:], xin[row:row + 128, :], [], [xrk, xrk + "b", xrk + "c"], xrk)
            return r

        def outproj_pre(t0):
            return [xres_load(t0, 0), xres_load(t0, 1)]

        def outproj_finish(t0, i, slots, bks, last=False):
            r = slots[i]
            if last and i >= 2:
                xr = xs[i - 2][:]
                xrk = "xs%d" % (i - 2)
            else:
                xr = wst[r][:]
                xrk = "wst%d" % r
            for half, (bk, bkey) in enumerate(bks):
                P.op("dve", lambda e, bk=bk, half=half, xr=xr: e.tensor_tensor(
                    out=xr[:, half * 512:(half + 1) * 512], in0=bk[:], in1=xr[:, half * 512:(half + 1) * 512], op=ALU.add),
                    reads=[bkey, xrk], writes=[xrk])
            orow = t0 + i * 128 - HALO
            if last:
                dma("pool" if i % 2 == 0 else "sp", yout[orow:orow + 128, :], xr, [xrk], [], "yl%d" % i)
            else:
                dma("pool", yout[orow:orow + 128, :], xr, [xrk], [], "yo%d" % r)
                if i + 2 < 4:
                    slots.append(xres_load(t0, i + 2))

        def outproj_tile(t0, i, slots):
            bks = []
            for half in range(2):
                bk, bkey = new_bank()

                def fo(e, bk=bk, half=half):
                    for kc in range(8):
                        ins = e.matmul(bk[:], lhsT=yT[:, kc, i * 128:(i + 1) * 128], rhs=Wo[:, kc, half * 512:(half + 1) * 512],
                                       start=(kc == 0), stop=(kc == 7))
                    return ins
                P.op("pe", fo, reads=["yTa%d" % i, "Wo"] + ["yTc%d" % c for c in range(4)], writes=[bkey])
                bks.append((bk, bkey))
            outproj_finish(t0, i, slots, bks)

        def outproj_last(t0, tails):
            slots = outproj_pre(t0) + [2, 3]
            for i in (2, 3):
                row = t0 + i * 128
                dma("sp", xs[i - 2][:], xin[row:row + 128, :], [], ["xs%d" % (i - 2)], "xs%d" % (i - 2))
            allb = []
            for i in range(4):
                bks = []
                for half in range(2):
                    bk, bkey = new_bank()

                    def fo(e, bk=bk, half=half, i=i):
                        for kc in range(4):
                            ins = e.matmul(bk[:], lhsT=yT[:, kc, i * 128:(i + 1) * 128], rhs=Wo[:, kc, half * 512:(half + 1) * 512],
                                           start=(kc == 0), stop=False)
                        return ins
                    P.op("pe", fo, reads=["yTa%d" % i, "Wo"], writes=[bkey])
                    bks.append((bk, bkey))
                allb.append(bks)
            for c in range(4):
                tails[c]()

                def fo2(e, c=c):
                    for i in range(4):
                        for half, (bk, bkey) in enumerate(allb[i]):
                            ins = e.matmul(bk[:], lhsT=yT[:, 4 + c, i * 128:(i + 1) * 128], rhs=Wo[:, 4 + c, half * 512:(half + 1) * 512],
                                           start=False, stop=(c == 3))
                    return ins
                P.op("pe", fo2, reads=["yTc%d" % c, "Wo"], writes=[bkey for bks in allb for (bk, bkey) in bks])
            for i in range(4):
                outproj_finish(t0, i, slots, allb[i], last=True)

        def build_bias():
            P.op("dve", lambda e: e.tensor_copy(out=relb_hb[:], in_=relb[:]), reads=["relb"], writes=["relb_hb"])
            P.op("dve", lambda e: e.tensor_copy(out=sel[:], in_=relb_hb[:]), reads=["relb_hb"], writes=["sel"])
            P.op("dve", lambda e: e.tensor_tensor(out=sel[64:128, :], in0=relb[64:128, :], in1=sel[64:128, :], op=ALU.subtract),
                 reads=["relb", "sel"], writes=["sel"])
            for h in range(8):
                R, Rk = new_sq()
                P.op("dve", lambda e, h=h, R=R: e.tensor_scalar(out=R[:], in0=oh[:], scalar1=sel[:, h:h + 1], scalar2=None,
                                                                op0=ALU.mult), reads=["oh", "sel"], writes=[Rk])
                bk, bkey = new_bank()
                P.op("pe", lambda e, R=R, bk=bk: e.matmul(bk[:], lhsT=ones_b[:], rhs=R[:], start=True, stop=True),
                     reads=[Rk, "ones_b"], writes=[bkey])
                S, Sk = new_tmp()
                act(S[:], bk[:], ACTF.Copy, [bkey], [Sk])
                dma("sp", scr[:, h * 512:(h + 1) * 512], S[:], [Sk], ["scr%d" % h], "scr%d" % h)
            for g in range(2):
                for part in range(2):
                    src = bass.AP(scr_t, (g * 8 + part) * 256 + 127, [[4095, 128], [512, 4], [1, 128]])
                    dst = bias[:, g * 2 + part, :].rearrange("p (h q) -> p h q", h=4)
                    P.op("sp", lambda e, src=src, dst=dst: e.dma_start(out=dst, in_=src), reads=["scr%d" % h for h in range(8)],
                         writes=["bias%d" % (g * 2 + part)], dma_key="bias%d" % (g * 2 + part))
                    act(ebias[:, g * 2 + part, :], bias[:, g * 2 + part, :], ACTF.Exp, ["bias%d" % (g * 2 + part)] + YTK, ["ebias"])

        def build_diag(cs=range(4)):
            for c in cs:
                in0 = identf[:].rearrange("p (o c) -> p o c", o=1).broadcast_to([128, 31, 128])
                in1 = dww[:, c * 31:(c + 1) * 31].rearrange("p (j o) -> p j o", o=1).broadcast_to([128, 31, 128])
                P.op("dve", lambda e, c=c, in0=in0, in1=in1: e.tensor_tensor(out=diag[:, c * 31:(c + 1) * 31, :], in0=in0, in1=in1,
                                                                            op=ALU.mult),
                     reads=["identf", "dww"], writes=["diag%d" % c])

        def roll_v():
            P.op("pool", lambda e: e.tensor_copy(out=Vp0[:, 0, :], in_=Vp0[:, 4, :]), reads=["Vp0"], writes=["Vp0"])
            P.op("pool", lambda e: e.tensor_copy(out=Vp1[:, 0, :], in_=Vp1[:, 4, :]), reads=["Vp1"], writes=["Vp1"])

        groups = [(HALO + 512 * g, 512) for g in range(4)]
        hsA = loadA(0, HALO)
        for blk in (4, 5):
            load_w_in_block(blk, None)
        loadB(hsA, dst=hTh, dkey="hTh")
        early_g1 = loadA_early(groups[0][0])
        hsG = loadA_late(groups[0][0], early_g1)
        for blk in (10, 11, 12, 13, 14, 15, 16, 17):
            load_w_in_block(blk, None)
        loadB(hsG)
        cur["hT"], cur["hk"] = hTh, "hTh"
        proj(0, HALO, full=False)
        cur["hT"], cur["hk"] = hT, "hT"
        for blk in (0, 1, 2, 3, 6, 7, 8, 9, 18, 19, 20, 21):
            load_w_in_block(blk, None)
        late_consts()
        early = loadA_early(groups[1][0])

        def bias_and_diag0():
            build_bias()
            build_diag([0])
        proj(*groups[0], full=True, tails=[bias_and_diag0] + [lambda c=c: build_diag([c]) for c in (1, 2, 3)], q_first=True)
        for blk in range(8):
            load_w_out_block(blk)
        late_const_ops()
        box = {}

        def g1_extra(c):
            conv_chunk(c)
            if c == 1:
                t0n = groups[1][0]
                box["hs"] = [x_chain(t0n, 0, early[0]), x_chain(t0n, 1, early[1])]
                box["r23"] = [x_dma(t0n, 2), x_dma(t0n, 3)]
            if c == 2:
                t0n = groups[1][0]
                box["hs"].append(x_chain(t0n, 2, box["r23"][0]))
                box["hs"].append(x_chain(t0n, 3, box["r23"][1]))
        attn(groups[0][0], extra=[lambda c=c: g1_extra(c) for c in range(4)])
        hs_next = box["hs"]
        early = None
        for gi in range(4):
            if early is not None:
                hs_next = loadA_late(groups[gi + 1][0], early)
                early = None
            tails = conv(groups[gi][0], chunks_done=(gi == 0), mid=(lambda: loadB(hs_next)) if gi + 1 < 4 else None)
            if gi + 1 < 4:
                roll_v()
                proj(*groups[gi + 1], full=True, tails=tails, q_first=True)
                pre = outproj_pre(groups[gi][0])
                if gi + 2 < 4:
                    early = loadA_early(groups[gi + 2][0])
                attn(groups[gi + 1][0], outproj_of=(groups[gi][0], pre))
            else:
                outproj_last(groups[gi][0], tails)

        if debug:
            for name, t, shape, dt in (("d_kT", kT, [128, NTOK], BF16), ("d_qT", qT, [128, 2048], BF16),
                                       ("d_zaT", zaT, [128, 2048], BF16), ("d_uT", uT, [128, 4 * 544], BF16),
                                       ("d_yT", yT, [128, 4096], BF16), ("d_Vp0", Vp0, [128, 5 * 128], BF16),
                                       ("d_rs", rs, [128, 20], F32),
                                       ("d_zcT", zcT, [128, 2048], BF16)):
                dd = nc.dram_tensor(name, shape, dt, kind="ExternalOutput")
                flat = t[:]
                if len(t[:].shape) == 3:
                    flat = t[:].rearrange("p a b -> p (a b)")
                P.op("pool", lambda e, dd=dd, flat=flat: e.dma_start(out=dd.ap(), in_=flat),
                     reads=["kT", "qT", "zaT", "uT"] + YTK + ["Vp0", "rs", "zcT"],
                     dma_key=name)

        P.final_waits("pool")
        print("sbuf bytes remaining", nc.sbuf_bytes_remaining, "sems", len(P.count))
        sems = {}
        for sk in list(P.count.keys()):
            sems[sk] = st.enter_context(nc.semaphore("s_" + "_".join(str(s) for s in sk)))
        P.emit(nc, sems)
    return nc


def _t5_bucket(n):
    n = np.maximum(n, 0)
    nf = np.maximum(n, 1).astype(np.float32)
    large = 16 + (np.log(nf / np.float32(16)) / np.float32(math.log(128 / 16)) * np.float32(16)).astype(np.int32)
    large = np.minimum(large, 31)
    return np.where(n < 16, n, large)


def _onehot_const():
    oh = np.zeros((64, 512), np.float32)
    for part in range(2):
        for j in range(255):
            u = j - 127
            if part == 0:
                valid = u < 0
                dist = u + 128
            else:
                valid = u >= 0
                dist = u
            col = part * 256 + j
            if valid:
                oh[int(_t5_bucket(np.array(dist))), col] = 1.0
            else:
                oh[32, col] = NEG
    return oh


def kernel(x, norm_w, w_in, q_norm_w, k_norm_w, sinks, dw_w, dw_b, ln_w, ln_b, w_out, rel_bias):
    x = np.asarray(x, np.float32)
    f = lambda a: np.ascontiguousarray(np.asarray(a, np.float32))
    norm_w, w_in, q_norm_w, k_norm_w, sinks = f(norm_w)[0], f(w_in)[0], f(q_norm_w)[0], f(k_norm_w)[0], f(sinks)[0]
    dw_w, dw_b, ln_w, ln_b, w_out, rel_bias = f(dw_w)[0], f(dw_b)[0], f(ln_w)[0], f(ln_b)[0], f(w_out)[0], f(rel_bias)

    nw = np.ascontiguousarray(norm_w.reshape(8, 128).T)
    qkw = np.ascontiguousarray(np.stack([np.tile(q_norm_w, 2), np.tile(k_norm_w, 2)], axis=1))
    sinks_t = np.ascontiguousarray(np.repeat(sinks.reshape(2, 1, 4), 64, axis=1).reshape(128, 4))
    dww = np.ascontiguousarray(dw_w.T.reshape(4, 128, 31).transpose(1, 0, 2).reshape(128, 124))
    cvec = np.ascontiguousarray(np.concatenate([v.reshape(4, 128).T for v in (dw_b, ln_w, ln_b)], axis=1))
    relb = np.zeros((128, 8), np.float32)
    relb[:32] = rel_bias
    relb[32] = 1.0
    relb[64:] = relb[:64]
    oh = np.concatenate([_onehot_const(), _onehot_const()], axis=0)
    identf = np.eye(128, dtype=np.float32)

    def perm_cols(base):
        idx = np.arange(512).reshape(2, 4, 64).transpose(1, 0, 2).reshape(-1)
        return base + idx
    cols = np.arange(2816)
    cols[0:512] = perm_cols(0)
    cols[768:1280] = perm_cols(768)
    w_in_p = w_in[:, cols]
    w_in_tiled = np.ascontiguousarray(w_in_p.reshape(8, 128, 22, 128).transpose(2, 1, 0, 3)).reshape(22 * 128, 1024)
    rows = np.arange(1024)
    rows[0:512] = np.arange(512).reshape(2, 4, 64).transpose(1, 0, 2).reshape(-1)
    w_out_p = w_out[rows, :]
    w_out_tiled = np.ascontiguousarray(w_out_p.reshape(8, 128, 8, 128).transpose(2, 1, 0, 3)).reshape(8 * 128, 1024)

    in_maps = []
    for c in range(NCORES):
        b, half = c // 2, c % 2
        xin = np.zeros((NTOK, 1024), np.float32)
        if half == 1:
            xin[:HALO] = x[b, SEQ_PER_CORE - HALO:SEQ_PER_CORE]
        xin[HALO:] = x[b, half * SEQ_PER_CORE:(half + 1) * SEQ_PER_CORE]
        pm = np.full((128, 1), NEG if half == 0 else 0.0, np.float32)
        in_maps.append({"xin": xin, "w_in": w_in_tiled, "w_out": w_out_tiled, "nw": nw, "qkw": qkw, "sinks_t": sinks_t, "dww": dww,
                        "cvec": cvec, "relb": relb, "oh": oh, "identf": identf, "pm": pm})
    nc = build_program(debug=DEBUG)
    res = run_bass_kernel_spmd(nc, in_maps, core_ids=list(range(NCORES)))
    out = np.empty((4, 4096, 1024), np.float32)
    for c in range(NCORES):
        b, half = c // 2, c % 2
        out[b, half * SEQ_PER_CORE:(half + 1) * SEQ_PER_CORE] = res.results[c]["y"]
    if DEBUG:
        kernel.debug_results = res.results
    return out
```

```python
import math
from contextlib import ExitStack

import numpy as np
import ml_dtypes

import concourse.bass as bass
import concourse.mybir as mybir
from concourse.bass_utils import run_bass_kernel_spmd

F32 = mybir.dt.float32
BF16 = mybir.dt.bfloat16
ALU = mybir.AluOpType
ACTF = mybir.ActivationFunctionType

ENGS = ("pe", "act", "dve", "pool", "sp")
NEG = -30000.0
NCORES = 8
SEQ_PER_CORE = 2048
HALO = 128
NTOK = SEQ_PER_CORE + HALO
DEBUG = False


class Plan:
    def __init__(self):
        self.streams = {e: [] for e in ENGS}
        self.count = {}
        self.waited = {e: {} for e in ENGS}
        self.last_w = {}
        self.readers = {}

    def _need(self, eng, dep, waits):
        if dep is None:
            return
        sk, val = dep
        if eng == "pe" and sk == ("eng", "pe"):
            return
        if self.waited[eng].get(sk, 0) >= val:
            return
        self.waited[eng][sk] = val
        for w in waits:
            if w[0] == sk:
                w[1] = max(w[1], val)
                return
        waits.append([sk, val])

    def op(self, eng, fn, reads=(), writes=(), dma_key=None):
        waits = []
        for k in reads:
            self._need(eng, self.last_w.get(k), waits)
        for k in writes:
            self._need(eng, self.last_w.get(k), waits)
            for r in self.readers.get(k, ()):
                self._need(eng, r, waits)
        if dma_key is not None:
            sk = ("dma", dma_key)
            inc = 16
        else:
            sk = ("eng", eng)
            inc = 1
        val = self.count.get(sk, 0) + inc
        self.count[sk] = val
        done = (sk, val)
        for k in reads:
            self.readers.setdefault(k, []).append(done)
        for k in writes:
            self.last_w[k] = done
            self.readers[k] = []
        self.streams[eng].append((waits, fn, sk, inc))
        return done

    def final_waits(self, eng):
        waits = []
        for sk, val in self.count.items():
            self._need(eng, (sk, val), waits)
        self.streams[eng].append((waits, None, None, 0))

    def emit(self, nc, sems):
        with nc.Block() as block:
            class Rec:
                def __init__(self, e):
                    self._e, self.first = e, None

                def __getattr__(self, name):
                    f = getattr(self._e, name)

                    def g(*a, **k):
                        r = f(*a, **k)
                        if self.first is None:
                            self.first = r
                        return r
                    return g

            def run(engname, e):
                for waits, fn, sk, inc in self.streams[engname]:
                    if fn is None:
                        for wsk, wval in waits:
                            e.wait_ge(sems[wsk], wval)
                        continue
                    for wsk, wval in waits[:-1]:
                        e.wait_ge(sems[wsk], wval)
                    rec = Rec(e)
                    ins = fn(rec)
                    if waits:
                        wsk, wval = waits[-1]
                        rec.first._wait_ge(sems[wsk], wval)
                    ins.then_inc(sems[sk], inc)

            @block.tensor
            def _(e):
                run("pe", e)

            @block.scalar
            def _(e):
                run("act", e)

            @block.vector
            def _(e):
                run("dve", e)

            @block.gpsimd
            def _(e):
                run("pool", e)

            @block.sync
            def _(e):
                run("sp", e)


def build_program(debug=False):
    nc = bass.Bass("TRN2", target_bir_lowering=False)
    D = lambda name, shape, dt, kind: nc.dram_tensor(name, shape, dt, kind=kind)
    xin_t = D("xin", [NTOK, 1024], F32, "ExternalInput")
    w_in_t = D("w_in", [22 * 128, 1024], F32, "ExternalInput")
    w_out_t = D("w_out", [8 * 128, 1024], F32, "ExternalInput")
    nw_t = D("nw", [128, 8], F32, "ExternalInput")
    qkw_t = D("qkw", [128, 2], F32, "ExternalInput")
    sinks_t = D("sinks_t", [128, 4], F32, "ExternalInput")
    dww_t = D("dww", [128, 4 * 31], F32, "ExternalInput")
    cvec_t = D("cvec", [128, 12], F32, "ExternalInput")
    relb_t = D("relb", [128, 8], F32, "ExternalInput")
    oh_t = D("oh", [128, 512], F32, "ExternalInput")
    ident_t = D("identf", [128, 128], F32, "ExternalInput")
    pm_t = D("pm", [128, 1], F32, "ExternalInput")
    y_t = D("y", [SEQ_PER_CORE, 1024], F32, "ExternalOutput")
    scr_t = D("scr", [128, 4096], F32, "Internal")
    xin, w_in, w_out, yout, scr = xin_t.ap(), w_in_t.ap(), w_out_t.ap(), y_t.ap(), scr_t.ap()

    P = Plan()
    dbg = {}
    with ExitStack() as st:
        def T(name, shape, dt):
            return st.enter_context(nc.sbuf_tensor(name, shape, dt))

        Wp = T("Wp", [128, 8, 2816], BF16)
        Wo = T("Wo", [128, 8, 1024], BF16)
        diag = T("diag", [128, 4 * 31, 128], BF16)
        wst = [T("wst%d" % i, [128, 1024], F32) for i in range(2)]
        xs = [T("xs%d" % i, [128, 1024], F32) for i in range(2)]
        hb = [T("hb%d" % i, [128, 1024], BF16) for i in range(4)]
        hT = T("hT", [128, 8, 512], BF16)
        hTh = T("hTh", [128, 8, 128], BF16)
        cur = {"hT": hT, "hk": "hT"}
        kT = T("kT", [128, NTOK], BF16)
        Vp0 = T("Vp0", [128, 5, 128], BF16)
        Vp1 = T("Vp1", [128, 5, 128], BF16)
        uT = T("uT", [128, 4, 544], BF16)
        qT = T("qT", [128, 4, 512], BF16)
        zaT = T("zaT", [128, 4, 512], BF16)
        zcT = T("zcT", [128, 4, 512], BF16)
        yT = T("yT", [128, 8, 512], BF16)
        ebias = T("ebias", [128, 4, 512], BF16)
        bias = yT[:].rearrange("p a b -> p (a b)").bitcast(F32).rearrange("p (a b) -> p a b", a=4)
        YTK = ["yTa0", "yTa1", "yTa2", "yTa3", "yTc0", "yTc1", "yTc2", "yTc3"]
        PT = [T("PT%d" % i, [128, 2, 512], BF16) for i in range(4)]
        NTMP = 9
        tmp = [T("tmp%d" % i, [128, 512], F32) for i in range(NTMP)]
        sqb = [T("sqb%d" % i, [128, 512], BF16) for i in range(2)]
        ln_mean = T("ln_mean", [128, 512], F32)
        ln_rstd = T("ln_rstd", [128, 512], F32)
        ycb = T("ycb", [128, 4, 512], BF16)
        ysq = T("ysq", [128, 4, 512], BF16)
        nw = T("nw_s", [128, 8], F32)
        qkw = T("qkw_s", [128, 2], F32)
        wq8 = T("wq8", [128, 1], F32)
        sink_s = T("sink_s", [128, 4], F32)
        esink = T("esink", [128, 4], F32)
        dww = T("dww_s", [128, 4 * 31], F32)
        cvec = T("cvec_s", [128, 12], F32)
        ncv = T("ncv", [128, 12], F32)
        relb = T("relb_s", [128, 8], F32)
        relb_hb = T("relb_hb", [128, 8], BF16)
        sel = T("sel", [128, 8], F32)
        oh = T("oh_s", [128, 512], F32)
        ones_b = T("ones_b", [128, 128], BF16)
        identf = T("identf_s", [128, 128], F32)
        identb = T("identb", [128, 128], BF16)
        pm = T("pm_s", [128, 1], F32)
        onesblk = T("onesblk", [128, 128], BF16)
        ones512 = T("ones512", [128, 128], BF16)
        od0 = T("od0", [128, 128], BF16)
        od1 = T("od1", [128, 128], BF16)
        ss = T("ss", [128, 20], F32)
        rsl = T("rsl", [128, 20], F32)
        rs = T("rs", [128, 20], F32)
        epsq = T("epsq", [128, 1], F32)
        epsl = T("epsl", [128, 1], F32)
        epsx = T("epsx", [128, 1], F32)
        onec = T("onec", [128, 1], F32)

        banks = [st.enter_context(nc.psum_tensor("ps%d" % i, [128, 512], F32)) for i in range(8)]
        bank_ctr = [0]

        def new_bank():
            i = bank_ctr[0] % 8
            bank_ctr[0] += 1
            return banks[i], "ps%d" % i

        tmp_ctr = [0]

        def new_tmp():
            i = tmp_ctr[0] % NTMP
            tmp_ctr[0] += 1
            return tmp[i], "tmp%d" % i

        sq_ctr = [0]

        def new_sq():
            i = sq_ctr[0] % 2
            sq_ctr[0] += 1
            return sqb[i], "sqb%d" % i

        def dma(eng, out, in_, reads, writes, key):
            P.op(eng, lambda e: e.dma_start(out=out, in_=in_), reads=reads, writes=writes, dma_key=key)

        def act(out, in_, func, reads, writes, bias=None, scale=None, accum_out=None):
            kw = {}
            if bias is not None:
                kw["bias"] = bias
            if scale is not None:
                kw["scale"] = scale
            if accum_out is not None:
                kw["accum_out"] = accum_out
            P.op("act", lambda e: e.activation(out=out, in_=in_, func=func, **kw), reads=reads, writes=writes)

        def sigmoid_chain(src, skey, extra_reads=(), scale=-1.0, bias=None, n=512):
            t, tk = new_tmp()
            act(t[:, 0:n], src, ACTF.Exp, [skey] + list(extra_reads), [tk], scale=scale, bias=bias)
            act(t[:, 0:n], t[:, 0:n], ACTF.Ln, [tk], [tk], bias=onec[:, 0:1])
            act(t[:, 0:n], t[:, 0:n], ACTF.Exp, [tk], [tk], scale=-1.0)
            return t, tk

        def rstd_chain(src, skey, epsap, n=512):
            t, tk = new_tmp()
            act(t[:, 0:n], src, ACTF.Ln, [skey], [tk], bias=epsap)
            act(t[:, 0:n], t[:, 0:n], ACTF.Exp, [tk], [tk], scale=-0.5)
            return t, tk

        for (dst, src, k) in ((nw, nw_t, "nw"), (identf, ident_t, "identf"), (qkw, qkw_t, "qkw")):
            dma("sp", dst[:], src.ap(), [], [k], k)

        def late_consts():
            for (dst, src, k) in ((sink_s, sinks_t, "sink_s"), (dww, dww_t, "dww"), (cvec, cvec_t, "cvec"), (relb, relb_t, "relb"),
                                  (oh, oh_t, "oh"), (pm, pm_t, "pm")):
                dma("sp", dst[:], src.ap(), [], [k], k)

        def late_const_ops():
            P.op("dve", lambda e: e.tensor_scalar(out=ncv[:], in0=cvec[:], scalar1=-1.0, scalar2=None, op0=ALU.mult),
                 reads=["cvec"], writes=["ncv"])
            act(esink[:], sink_s[:], ACTF.Exp, ["sink_s"], ["esink"])

        def ms(t, val, key, ap=None):
            a = t[:] if ap is None else ap
            P.op("pool", lambda e: e.memset(a, val), writes=[key])

        ms(ones_b, 1.0, "ones_b")
        ms(onesblk, 0.0, "onesblk")
        ms(onesblk, 1.0 / 64, "onesblk", onesblk[0:64, 0:64])
        ms(onesblk, 1.0 / 64, "onesblk", onesblk[64:128, 64:128])
        ms(ones512, 1.0 / 512, "ones512")
        ms(od0, 0.0, "od0")
        ms(od0, 1.0, "od0", od0[:, 0:64])
        ms(od1, 0.0, "od1")
        ms(od1, 1.0, "od1", od1[:, 64:128])
        ms(Vp0, 0.0, "Vp0")
        ms(Vp1, 0.0, "Vp1")
        ms(epsq, 1e-6, "epsq")
        ms(epsl, 1e-5, "epsl")
        ms(epsx, 1e-6, "epsx")
        ms(onec, 1.0, "onec")
        ms(ss, 0.0, "ss")
        P.op("dve", lambda e: e.tensor_copy(out=identb[:], in_=identf[:]), reads=["identf"], writes=["identb"])
        P.op("dve", lambda e: e.tensor_scalar(out=wq8[:], in0=qkw[:, 0:1], scalar1=0.125, scalar2=None, op0=ALU.mult),
             reads=["qkw"], writes=["wq8"])

        xs_ctr = [0]

        hb_ctr = [0]

        def x_dma(t0, i):
            r = xs_ctr[0] % 2
            xs_ctr[0] += 1
            dma("sp", xs[r][:], xin[t0 + i * 128:t0 + (i + 1) * 128, :], [], ["xs%d" % r], "xs%d" % r)
            return r

        def x_chain(t0, i, r):
            rh = hb_ctr[0] % 4
            hb_ctr[0] += 1
            idx = (t0 // 128) + i
            xk, hk = "xs%d" % r, "hb%d" % rh
            act(hb[rh][:], xs[r][:], ACTF.Square, [xk], [hk, "ss"], accum_out=ss[:, idx:idx + 1])
            act(rsl[:, idx:idx + 1], ss[:, idx:idx + 1], ACTF.Ln, ["ss"], ["rsl"], scale=1.0 / 1024, bias=epsx[:, 0:1])
            act(rs[:, idx:idx + 1], rsl[:, idx:idx + 1], ACTF.Exp, ["rsl"], ["rs"], scale=-0.5)
            P.op("dve", lambda e, r=r, rh=rh, idx=idx: e.tensor_scalar(out=hb[rh][:], in0=xs[r][:], scalar1=rs[:, idx:idx + 1],
                                                                       scalar2=None, op0=ALU.mult),
                 reads=[xk, "rs"], writes=[hk])
            return rh

        def loadA(t0, ntok):
            hs = []
            for i in range(ntok // 128):
                hs.append(x_chain(t0, i, x_dma(t0, i)))
            return hs

        def loadA_early(t0):
            return [x_dma(t0, 0), x_dma(t0, 1)]

        def loadA_late(t0, rs01):
            hs = [x_chain(t0, 0, rs01[0]), x_chain(t0, 1, rs01[1])]
            r2 = x_dma(t0, 2)
            r3 = x_dma(t0, 3)
            hs.append(x_chain(t0, 2, r2))
            hs.append(x_chain(t0, 3, r3))
            return hs

        def loadB(hs, dst=None, dkey="hT"):
            dst = hT if dst is None else dst
            for i, rh in enumerate(hs):
                hk = "hb%d" % rh
                bk, bkey = new_bank()
                bkb = bk[:].bitcast(BF16)

                def tr(e, rh=rh, bkb=bkb):
                    for kc in range(8):
                        ins = e.transpose(out=bkb[:, kc * 128:(kc + 1) * 128], in_=hb[rh][:, kc * 128:(kc + 1) * 128],
                                          identity=identb[:])
                    return ins
                P.op("pe", tr, reads=[hk, "identb"], writes=[bkey])
                P.op("dve", lambda e, i=i, bkb=bkb: e.tensor_copy(out=dst[:, :, i * 128:(i + 1) * 128],
                                                                  in_=bkb.rearrange("p (k t) -> p k t", k=8)),
                     reads=[bkey], writes=[dkey])

        w_in_r = w_in.rearrange("(k p) c -> p k c", p=128)
        wst_ctr = [0]
        wo_ctr = [0]

        def f32view(t):
            return t[:].rearrange("p a b -> p (a b)").bitcast(F32)
        wslots = [(wst[0][:], ["wst0"]), (wst[1][:], ["wst1"]), (f32view(zaT), ["zaT"]), (f32view(zcT), ["zcT"]),
                  (f32view(qT), ["qT"]), (f32view(ycb), ["ycb%d" % c for c in range(4)]),
                  (f32view(ysq), ["ysq%d" % c for c in range(4)])]

        def load_w_in_block(blk, permute):
            r = wst_ctr[0] % len(wslots)
            wst_ctr[0] += 1
            stage, wkeys_ = wslots[r]
            st3 = stage.rearrange("p (k c) -> p k c", k=8)
            dma("sp", stage, w_in[blk * 128:(blk + 1) * 128, :], [], wkeys_, "wslot%d" % r)
            out = Wp[:, :, blk * 128:(blk + 1) * 128]
            nwb = nw[:].rearrange("p (k o) -> p k o", o=1).broadcast_to([128, 8, 128])
            P.op("dve", lambda e: e.tensor_tensor(out=out, in0=st3, in1=nwb, op=ALU.mult),
                 reads=wkeys_ + ["nw"], writes=["Wp%d" % blk])

        def load_w_out_block(blk):
            r = wo_ctr[0] % 2
            wo_ctr[0] += 1
            wk = "wst%d" % r
            st3 = wst[r][:].rearrange("p (k c) -> p k c", k=8)
            cs = slice(blk * 128, (blk + 1) * 128)
            subkeys = [wk, wk + "b", wk + "c"]
            dma("sp", wst[r][:], w_out[blk * 128:(blk + 1) * 128, :], [], subkeys, wk)
            P.op("pool", lambda e: e.tensor_copy(out=Wo[:, :, cs], in_=st3), reads=subkeys, writes=["Wo"])

        COL_K, COL_V, COL_ZA, COL_A, COL_G, COL_ZC = 512, 640, 768, 1280, 1792, 2304

        def wkeys(c0):
            return ["Wp%d" % (c0 // 128)]

        def fm_matmul(c0, n, t_lo=0):
            bk, bkey = new_bank()

            src, skey = cur["hT"], cur["hk"]

            def f(e):
                for kc in range(8):
                    ins = e.matmul(bk[:, 0:n - t_lo], lhsT=Wp[:, kc, c0:c0 + 128], rhs=src[:, kc, t_lo:n], start=(kc == 0), stop=(kc == 7))
                return ins
            P.op("pe", f, reads=[skey] + wkeys(c0), writes=[bkey])
            return bk, bkey

        def qk_stage1(c0, n, wcol, out_ap, out_key):
            bk, bkey = fm_matmul(c0, n)
            sq, sqk = new_sq()
            act(sq[:, 0:n], bk[:, 0:n], ACTF.Square, [bkey], [sqk])
            return (bk, bkey, sq, sqk, n, wcol, out_ap, out_key)

        def qk_stage2(stt):
            bk, bkey, sq, sqk, n, wcol, out_ap, out_key = stt
            b2, b2k = new_bank()
            P.op("pe", lambda e: e.matmul(b2[:, 0:n], lhsT=onesblk[:], rhs=sq[:, 0:n], start=True, stop=True),
                 reads=[sqk, "onesblk"], writes=[b2k])
            rt, rtk = rstd_chain(b2[:, 0:n], b2k, epsq[:, 0:1], n)
            P.op("dve", lambda e: e.scalar_tensor_tensor(out=out_ap, in0=bk[:, 0:n], scalar=wcol, in1=rt[:, 0:n],
                                                         op0=ALU.mult, op1=ALU.mult),
                 reads=[bkey, rtk, "qkw", "wq8"], writes=[out_key])

        def proj(t0, n, full, tails=(), tails_per=1, q_first=False):
            tails = list(tails)
            qpend = [None]

            def q_chunks():
                pend = None
                for hh in range(4):
                    stt = qk_stage1(hh * 128, n, wq8[:, 0:1], qT[:, hh, 0:n], "qT")
                    if pend is not None:
                        qk_stage2(pend)
                    pend = stt
                qpend[0] = pend

            def q_flush():
                if qpend[0] is not None:
                    qk_stage2(qpend[0])
                    qpend[0] = None

            def pop_tail(k=1):
                for _ in range(k):
                    if tails:
                        tails.pop(0)()
            tile0 = 1 if full else 0
            kst = qk_stage1(COL_K, n, qkw[:, 1:2], kT[:, t0:t0 + n], "kT")
            bk, bkey = new_bank()
            nt = n // 128

            vsrc, vkey = cur["hT"], cur["hk"]

            def fv(e):
                for i in range(nt):
                    for kc in range(8):
                        ins = e.matmul(bk[:, i * 128:(i + 1) * 128], lhsT=vsrc[:, kc, i * 128:(i + 1) * 128],
                                       rhs=Wp[:, kc, COL_V:COL_V + 128], start=(kc == 0), stop=(kc == 7))
                return ins
            P.op("pe", fv, reads=[vkey] + wkeys(COL_V), writes=[bkey])
            qk_stage2(kst)
            bk3 = bk[:, 0:n].rearrange("p (i c) -> p i c", c=128)
            P.op("dve", lambda e: e.tensor_copy(out=Vp0[:, tile0:tile0 + nt, 0:64], in_=bk3[:, :, 0:64]), reads=[bkey], writes=["Vp0"])
            P.op("dve", lambda e: e.tensor_copy(out=Vp1[:, tile0:tile0 + nt, 64:128], in_=bk3[:, :, 64:128]), reads=[bkey],
                 writes=["Vp1"])
            if full and q_first:
                q_chunks()
            for c in range(4):
                if not full:
                    ba, bak = fm_matmul(COL_A + c * 128, n, t_lo=n - 32)
                    bg, bgk = fm_matmul(COL_G + c * 128, n, t_lo=n - 32)
                    s, sk = sigmoid_chain(bg[:, 0:32], bgk, n=32)
                    P.op("dve", lambda e, c=c, ba=ba, s=s: e.tensor_tensor(out=uT[:, c, 0:32], in0=ba[:, 0:32], in1=s[:, 0:32], op=ALU.mult),
                         reads=[bak, sk], writes=["uT"])
                    pop_tail(tails_per)
                    continue
                ba, bak = fm_matmul(COL_A + c * 128, n)
                q_flush()
                bg, bgk = fm_matmul(COL_G + c * 128, n)
                s, sk = sigmoid_chain(bg[:, 0:n], bgk, n=n)
                if full:
                    P.op("dve", lambda e, c=c, ba=ba, s=s: e.tensor_tensor(out=uT[:, c, 32:32 + n], in0=ba[:, 0:n], in1=s[:, 0:n],
                                                                          op=ALU.mult),
                         reads=[bak, sk], writes=["uT"])
                else:
                    P.op("dve", lambda e, c=c, ba=ba, s=s: e.tensor_tensor(out=uT[:, c, 0:32], in0=ba[:, n - 32:n],
                                                                          in1=s[:, n - 32:n], op=ALU.mult),
                         reads=[bak, sk], writes=["uT"])
                pop_tail(tails_per)
            while tails:
                pop_tail()
            if full:
                def gate_chunk(col, dst, dk, c):
                    bz, bzk = fm_matmul(col + c * 128, n)
                    s, sk = sigmoid_chain(bz[:, 0:n], bzk, n=n)
                    P.op("dve", lambda e: e.tensor_tensor(out=dst[:, c, 0:n], in0=bz[:, 0:n], in1=s[:, 0:n], op=ALU.mult),
                         reads=[bzk, sk], writes=[dk])
                if not q_first:
                    q_chunks()
                gate_chunk(COL_ZA, zaT, "zaT", 0)
                q_flush()
                for c in range(1, 4):
                    gate_chunk(COL_ZA, zaT, "zaT", c)
                for c in range(4):
                    gate_chunk(COL_ZC, zcT, "zcT", c)

        pt_ctr = [0]
        esink_b = esink[:].rearrange("p (h o) -> p h o", o=1).broadcast_to([128, 4, 128])

        def attn_qk(t0, b):
            own = t0 + b * 128
            first = (own == HALO)
            rr = []
            for kvg in range(2):
                rr.append(pt_ctr[0] % 4)
                pt_ctr[0] += 1
            for part in range(2):
                k0 = own - 128 + part * 128
                bks = [new_bank(), new_bank()]

                def fqk(e, bks=bks, k0=k0):
                    for hh in range(4):
                        for kvg in range(2):
                            ps = slice(kvg * 64, (kvg + 1) * 64)
                            ins = e.matmul(bks[kvg][0][:, hh * 128:(hh + 1) * 128], lhsT=kT[ps, k0:k0 + 128],
                                           rhs=qT[ps, hh, b * 128:(b + 1) * 128], start=True, stop=True)
                    return ins
                P.op("pe", fqk, reads=["kT", "qT"], writes=[bks[0][1], bks[1][1]])
                for kvg in range(2):
                    r = rr[kvg]
                    ptk = "PT%d" % r
                    bk, bkey = bks[kvg]
                    if first and part == 0:
                        act(PT[r][:, part, :], bk[:], ACTF.Exp, [bkey, "pm"], [ptk], bias=pm[:, 0:1])
                    else:
                        act(PT[r][:, part, :], bk[:], ACTF.Exp, [bkey], [ptk])
                    P.op("dve", lambda e, r=r, kvg=kvg, part=part: e.tensor_tensor(
                        out=PT[r][:, part, :], in0=PT[r][:, part, :], in1=ebias[:, kvg * 2 + part, :], op=ALU.mult),
                        reads=[ptk, "ebias"], writes=[ptk])
            return [(PT[rr[0]], "PT%d" % rr[0]), (PT[rr[1]], "PT%d" % rr[1])]

        def attn_pv(t0, b, pts):
            tile_own = 1 + b
            bn, bnk = new_bank()
            bd, bdk = new_bank()

            def fpv(e):
                i = 0
                for kvg in range(2):
                    Vp = Vp0 if kvg == 0 else Vp1
                    for part in range(2):
                        ins = e.matmul(bn[:], lhsT=Vp[:, tile_own - 1 + part, :], rhs=pts[kvg][0][:, part, :],
                                       start=(i == 0), stop=(i == 3))
                        i += 1
                return ins
            P.op("pe", fpv, reads=["Vp0", "Vp1", pts[0][1], pts[1][1]], writes=[bnk])

            def fden(e):
                i = 0
                for kvg in range(2):
                    od = od0 if kvg == 0 else od1
                    for part in range(2):
                        ins = e.matmul(bd[:], lhsT=od[:], rhs=pts[kvg][0][:, part, :], start=(i == 0), stop=(i == 3))
                        i += 1
                return ins
            P.op("pe", fden, reads=["od0", "od1", pts[0][1], pts[1][1]], writes=[bdk])
            dt_, dtk = new_tmp()
            dt3 = dt_[:].rearrange("p (h q) -> p h q", h=4)
            P.op("dve", lambda e: e.tensor_tensor(out=dt3, in0=bd[:].rearrange("p (h q) -> p h q", h=4), in1=esink_b, op=ALU.add),
                 reads=[bdk, "esink"], writes=[dtk])
            act(dt_[:], dt_[:], ACTF.Ln, [dtk], [dtk])
            act(dt_[:], dt_[:], ACTF.Exp, [dtk], [dtk], scale=-1.0)
            P.op("dve", lambda e: e.tensor_tensor(out=dt3, in0=dt3, in1=zaT[:, :, b * 128:(b + 1) * 128], op=ALU.mult),
                 reads=[dtk, "zaT"], writes=[dtk])
            P.op("dve", lambda e: e.tensor_tensor(out=yT[:, 0:4, b * 128:(b + 1) * 128], in0=bn[:].rearrange("p (h q) -> p h q", h=4),
                                                  in1=dt3, op=ALU.mult),
                 reads=[bnk, dtk], writes=["yTa%d" % b])

        def attn(t0, outproj_of=None, extra=()):
            prev = None
            extra = list(extra)
            slots = list(outproj_of[1]) if outproj_of is not None else None
            for b in range(4):
                pts = attn_qk(t0, b)
                if outproj_of is not None:
                    outproj_tile(outproj_of[0], b, slots)
                if extra:
                    extra.pop(0)()
                if prev is not None:
                    attn_pv(t0, prev[0], prev[1])
                prev = (b, pts)
            attn_pv(t0, prev[0], prev[1])

        def conv_chunk(c):
            if True:
                bk, bkey = new_bank()

                def fc(e, c=c, bk=bk):
                    for j in range(31):
                        ins = e.matmul(bk[:], lhsT=diag[:, c * 31 + j, :], rhs=uT[:, c, 2 + j:2 + j + 512],
                                       start=(j == 0), stop=(j == 30))
                    return ins
                P.op("pe", fc, reads=["uT", "diag%d" % c], writes=[bkey])
                act(ycb[:, c, :], bk[:], ACTF.Identity, [bkey, "cvec"], ["ycb%d" % c], bias=cvec[:, c:c + 1])
                act(ysq[:, c, :], bk[:], ACTF.Square, [bkey, "cvec"], ["ysq%d" % c], bias=cvec[:, c:c + 1])

        def conv(t0, chunks_done=False, mid=None):
            if not chunks_done:
                for c in range(4):
                    conv_chunk(c)
                    if c == 1 and mid is not None:
                        mid()
            elif mid is not None:
                mid()
            P.op("pool", lambda e: e.tensor_copy(out=uT[:, :, 0:32], in_=uT[:, :, 512:544]), reads=["uT"], writes=["uT"])
            bm, bmk = new_bank()
            be, bek = new_bank()

            def fm(e):
                for c in range(4):
                    ins = e.matmul(bm[:], lhsT=ones512[:], rhs=ycb[:, c, :], start=(c == 0), stop=(c == 3))
                return ins
            P.op("pe", fm, reads=["ones512"] + ["ycb%d" % c for c in range(4)], writes=[bmk])

            def fe(e):
                for c in range(4):
                    ins = e.matmul(be[:], lhsT=ones512[:], rhs=ysq[:, c, :], start=(c == 0), stop=(c == 3))
                return ins
            P.op("pe", fe, reads=["ones512"] + ["ysq%d" % c for c in range(4)], writes=[bek])
            mean, mk = ln_mean, "ln_mean"
            act(mean[:], bm[:], ACTF.Copy, [bmk], [mk])
            var, vk = new_tmp()
            P.op("dve", lambda e: e.tensor_tensor(out=var[:], in0=mean[:], in1=mean[:], op=ALU.mult), reads=[mk], writes=[vk])
            P.op("dve", lambda e: e.tensor_tensor(out=var[:], in0=be[:], in1=var[:], op=ALU.subtract), reads=[bek, vk], writes=[vk])
            rstd, rk = ln_rstd, "ln_rstd"
            act(rstd[:], var[:], ACTF.Ln, [vk], [rk], bias=epsl[:, 0:1])
            act(rstd[:], rstd[:], ACTF.Exp, [rk], [rk], scale=-0.5)

            def tail(c):
                n_, nk = new_tmp()
                P.op("pool", lambda e, c=c, n_=n_: e.tensor_tensor(out=n_[:], in0=ycb[:, c, :], in1=mean[:], op=ALU.subtract),
                     reads=["ycb%d" % c, mk], writes=[nk])
                P.op("pool", lambda e, n_=n_: e.tensor_tensor(out=n_[:], in0=n_[:], in1=rstd[:], op=ALU.mult),
                     reads=[nk, rk], writes=[nk])
                s, sk = sigmoid_chain(n_[:], nk, extra_reads=["ncv"], scale=ncv[:, 4 + c:5 + c], bias=ncv[:, 8 + c:9 + c])
                P.op("dve", lambda e, c=c, n_=n_: e.tensor_scalar(out=n_[:], in0=n_[:], scalar1=cvec[:, 4 + c:5 + c],
                                                                  scalar2=cvec[:, 8 + c:9 + c], op0=ALU.mult, op1=ALU.add),
                     reads=[nk, "cvec", sk], writes=[nk])
                P.op("dve", lambda e, n_=n_, s=s: e.tensor_tensor(out=n_[:], in0=n_[:], in1=s[:], op=ALU.mult),
                     reads=[nk, sk], writes=[nk])
                P.op("dve", lambda e, c=c, n_=n_: e.tensor_tensor(out=yT[:, 4 + c, :], in0=n_[:], in1=zcT[:, c, :], op=ALU.mult),
                     reads=[nk, "zcT"], writes=["yTc%d" % c])
            return [lambda c=c: tail(c) for c in range(4)]

        xr_ctr = [0]

        def xres_load(t0, i):
            r = xr_ctr[0] % 2
            xr_ctr[0] += 1
            xrk = "wst%d" % r
            row = t0 + i * 128
            dma("sp", wst[r][:], xin[row:row + 128, :], [], [xrk, xrk + "b", xrk + "c"], xrk)
            return r

        def outproj_pre(t0):
            return [xres_load(t0, 0), xres_load(t0, 1)]

        def outproj_finish(t0, i, slots, bks, last=False):
            r = slots[i]
            if last and i >= 2:
                xr = xs[i - 2][:]
                xrk = "xs%d" % (i - 2)
            else:
                xr = wst[r][:]
                xrk = "wst%d" % r
            for half, (bk, bkey) in enumerate(bks):
                P.op("dve", lambda e, bk=bk, half=half, xr=xr: e.tensor_tensor(
                    out=xr[:, half * 512:(half + 1) * 512], in0=bk[:], in1=xr[:, half * 512:(half + 1) * 512], op=ALU.add),
                    reads=[bkey, xrk], writes=[xrk])
            orow = t0 + i * 128 - HALO
            if last:
                dma("pool" if i % 2 == 0 else "sp", yout[orow:orow + 128, :], xr, [xrk], [], "yl%d" % i)
            else:
                dma("pool", yout[orow:orow + 128, :], xr, [xrk], [], "yo%d" % r)
                if i + 2 < 4:
                    slots.append(xres_load(t0, i + 2))

        def outproj_tile(t0, i, slots):
            bks = []
            for half in range(2):
                bk, bkey = new_bank()

                def fo(e, bk=bk, half=half):
                    for kc in range(8):
                        ins = e.matmul(bk[:], lhsT=yT[:, kc, i * 128:(i + 1) * 128], rhs=Wo[:, kc, half * 512:(half + 1) * 512],
                                       start=(kc == 0), stop=(kc == 7))
                    return ins
                P.op("pe", fo, reads=["yTa%d" % i, "Wo"] + ["yTc%d" % c for c in range(4)], writes=[bkey])
                bks.append((bk, bkey))
            outproj_finish(t0, i, slots, bks)

        def outproj_last(t0, tails):
            slots = outproj_pre(t0) + [2, 3]
            for i in (2, 3):
                row = t0 + i * 128
                dma("sp", xs[i - 2][:], xin[row:row + 128, :], [], ["xs%d" % (i - 2)], "xs%d" % (i - 2))
            allb = []
            for i in range(4):
                bks = []
                for half in range(2):
                    bk, bkey = new_bank()

                    def fo(e, bk=bk, half=half, i=i):
                        for kc in range(4):
                            ins = e.matmul(bk[:], lhsT=yT[:, kc, i * 128:(i + 1) * 128], rhs=Wo[:, kc, half * 512:(half + 1) * 512],
                                           start=(kc == 0), stop=False)
                        return ins
                    P.op("pe", fo, reads=["yTa%d" % i, "Wo"], writes=[bkey])
                    bks.append((bk, bkey))
                allb.append(bks)
            for c in range(4):
                tails[c]()

                def fo2(e, c=c):
                    for i in range(4):
                        for half, (bk, bkey) in enumerate(allb[i]):
                            ins = e.matmul(bk[:], lhsT=yT[:, 4 + c, i * 128:(i + 1) * 128], rhs=Wo[:, 4 + c, half * 512:(half + 1) * 512],
                                           start=False, stop=(c == 3))
                    return ins
                P.op("pe", fo2, reads=["yTc%d" % c, "Wo"], writes=[bkey for bks in allb for (bk, bkey) in bks])
            for i in range(4):
                outproj_finish(t0, i, slots, allb[i], last=True)

        def build_bias():
            P.op("dve", lambda e: e.tensor_copy(out=relb_hb[:], in_=relb[:]), reads=["relb"], writes=["relb_hb"])
            P.op("dve", lambda e: e.tensor_copy(out=sel[:], in_=relb_hb[:]), reads=["relb_hb"], writes=["sel"])
            P.op("dve", lambda e: e.tensor_tensor(out=sel[64:128, :], in0=relb[64:128, :], in1=sel[64:128, :], op=ALU.subtract),
                 reads=["relb", "sel"], writes=["sel"])
            for h in range(8):
                R, Rk = new_sq()
                P.op("dve", lambda e, h=h, R=R: e.tensor_scalar(out=R[:], in0=oh[:], scalar1=sel[:, h:h + 1], scalar2=None,
                                                                op0=ALU.mult), reads=["oh", "sel"], writes=[Rk])
                bk, bkey = new_bank()
                P.op("pe", lambda e, R=R, bk=bk: e.matmul(bk[:], lhsT=ones_b[:], rhs=R[:], start=True, stop=True),
                     reads=[Rk, "ones_b"], writes=[bkey])
                S, Sk = new_tmp()
                act(S[:], bk[:], ACTF.Copy, [bkey], [Sk])
                dma("sp", scr[:, h * 512:(h + 1) * 512], S[:], [Sk], ["scr%d" % h], "scr%d" % h)
            for g in range(2):
                for part in range(2):
                    src = bass.AP(scr_t, (g * 8 + part) * 256 + 127, [[4095, 128], [512, 4], [1, 128]])
                    dst = bias[:, g * 2 + part, :].rearrange("p (h q) -> p h q", h=4)
                    P.op("sp", lambda e, src=src, dst=dst: e.dma_start(out=dst, in_=src), reads=["scr%d" % h for h in range(8)],
                         writes=["bias%d" % (g * 2 + part)], dma_key="bias%d" % (g * 2 + part))
                    act(ebias[:, g * 2 + part, :], bias[:, g * 2 + part, :], ACTF.Exp, ["bias%d" % (g * 2 + part)] + YTK, ["ebias"])

        def build_diag(cs=range(4)):
            for c in cs:
                in0 = identf[:].rearrange("p (o c) -> p o c", o=1).broadcast_to([128, 31, 128])
                in1 = dww[:, c * 31:(c + 1) * 31].rearrange("p (j o) -> p j o", o=1).broadcast_to([128, 31, 128])
                P.op("dve", lambda e, c=c, in0=in0, in1=in1: e.tensor_tensor(out=diag[:, c * 31:(c + 1) * 31, :], in0=in0, in1=in1,
                                                                            op=ALU.mult),
                     reads=["identf", "dww"], writes=["diag%d" % c])

        def roll_v():
            P.op("pool", lambda e: e.tensor_copy(out=Vp0[:, 0, :], in_=Vp0[:, 4, :]), reads=["Vp0"], writes=["Vp0"])
            P.op("pool", lambda e: e.tensor_copy(out=Vp1[:, 0, :], in_=Vp1[:, 4, :]), reads=["Vp1"], writes=["Vp1"])

        groups = [(HALO + 512 * g, 512) for g in range(4)]
        hsA = loadA(0, HALO)
        for blk in (4, 5):
            load_w_in_block(blk, None)
        loadB(hsA, dst=hTh, dkey="hTh")
        early_g1 = loadA_early(groups[0][0])
        hsG = loadA_late(groups[0][0], early_g1)
        for blk in (10, 11, 12, 13, 14, 15, 16, 17):
            load_w_in_block(blk, None)
        loadB(hsG)
        cur["hT"], cur["hk"] = hTh, "hTh"
        proj(0, HALO, full=False)
        cur["hT"], cur["hk"] = hT, "hT"
        for blk in (0, 1, 2, 3, 6, 7, 8, 9, 18, 19, 20, 21):
            load_w_in_block(blk, None)
        late_consts()
        early = loadA_early(groups[1][0])

        def bias_and_diag0():
            build_bias()
            build_diag([0])
        proj(*groups[0], full=True, tails=[bias_and_diag0] + [lambda c=c: build_diag([c]) for c in (1, 2, 3)], q_first=True)
        for blk in range(8):
            load_w_out_block(blk)
        late_const_ops()
        box = {}

        def g1_extra(c):
            conv_chunk(c)
            if c == 1:
                t0n = groups[1][0]
                box["hs"] = [x_chain(t0n, 0, early[0]), x_chain(t0n, 1, early[1])]
                box["r23"] = [x_dma(t0n, 2), x_dma(t0n, 3)]
            if c == 2:
                t0n = groups[1][0]
                box["hs"].append(x_chain(t0n, 2, box["r23"][0]))
                box["hs"].append(x_chain(t0n, 3, box["r23"][1]))
            if c == 3:
                loadB(box["hs"])
        attn(groups[0][0], extra=[lambda c=c: g1_extra(c) for c in range(4)])
        hs_next = box["hs"]
        early = None
        for gi in range(4):
            if early is not None:
                hs_next = loadA_late(groups[gi + 1][0], early)
                early = None
            tails = conv(groups[gi][0], chunks_done=(gi == 0), mid=(lambda: loadB(hs_next)) if 0 < gi < 3 else None)
            if gi + 1 < 4:
                roll_v()
                proj(*groups[gi + 1], full=True, tails=tails, q_first=True)
                pre = outproj_pre(groups[gi][0])
                if gi + 2 < 4:
                    early = loadA_early(groups[gi + 2][0])
                attn(groups[gi + 1][0], outproj_of=(groups[gi][0], pre))
            else:
                outproj_last(groups[gi][0], tails)

        if debug:
            for name, t, shape, dt in (("d_kT", kT, [128, NTOK], BF16), ("d_qT", qT, [128, 2048], BF16),
                                       ("d_zaT", zaT, [128, 2048], BF16), ("d_uT", uT, [128, 4 * 544], BF16),
                                       ("d_yT", yT, [128, 4096], BF16), ("d_Vp0", Vp0, [128, 5 * 128], BF16),
                                       ("d_rs", rs, [128, 20], F32),
                                       ("d_zcT", zcT, [128, 2048], BF16)):
                dd = nc.dram_tensor(name, shape, dt, kind="ExternalOutput")
                flat = t[:]
                if len(t[:].shape) == 3:
                    flat = t[:].rearrange("p a b -> p (a b)")
                P.op("pool", lambda e, dd=dd, flat=flat: e.dma_start(out=dd.ap(), in_=flat),
                     reads=["kT", "qT", "zaT", "uT"] + YTK + ["Vp0", "rs", "zcT"],
                     dma_key=name)

        P.final_waits("pool")
        print("sbuf bytes remaining", nc.sbuf_bytes_remaining, "sems", len(P.count))
        sems = {}
        for sk in list(P.count.keys()):
            sems[sk] = st.enter_context(nc.semaphore("s_" + "_".join(str(s) for s in sk)))
        P.emit(nc, sems)
    return nc


def _t5_bucket(n):
    n = np.maximum(n, 0)
    nf = np.maximum(n, 1).astype(np.float32)
    large = 16 + (np.log(nf / np.float32(16)) / np.float32(math.log(128 / 16)) * np.float32(16)).astype(np.int32)
    large = np.minimum(large, 31)
    return np.where(n < 16, n, large)


def _onehot_const():
    oh = np.zeros((64, 512), np.float32)
    for part in range(2):
        for j in range(255):
            u = j - 127
            if part == 0:
                valid = u < 0
                dist = u + 128
            else:
                valid = u >= 0
                dist = u
            col = part * 256 + j
            if valid:
                oh[int(_t5_bucket(np.array(dist))), col] = 1.0
            else:
                oh[32, col] = NEG
    return oh


def kernel(x, norm_w, w_in, q_norm_w, k_norm_w, sinks, dw_w, dw_b, ln_w, ln_b, w_out, rel_bias):
    x = np.asarray(x, np.float32)
    f = lambda a: np.ascontiguousarray(np.asarray(a, np.float32))
    norm_w, w_in, q_norm_w, k_norm_w, sinks = f(norm_w)[0], f(w_in)[0], f(q_norm_w)[0], f(k_norm_w)[0], f(sinks)[0]
    dw_w, dw_b, ln_w, ln_b, w_out, rel_bias = f(dw_w)[0], f(dw_b)[0], f(ln_w)[0], f(ln_b)[0], f(w_out)[0], f(rel_bias)

    nw = np.ascontiguousarray(norm_w.reshape(8, 128).T)
    qkw = np.ascontiguousarray(np.stack([np.tile(q_norm_w, 2), np.tile(k_norm_w, 2)], axis=1))
    sinks_t = np.ascontiguousarray(np.repeat(sinks.reshape(2, 1, 4), 64, axis=1).reshape(128, 4))
    dww = np.ascontiguousarray(dw_w.T.reshape(4, 128, 31).transpose(1, 0, 2).reshape(128, 124))
    cvec = np.ascontiguousarray(np.concatenate([v.reshape(4, 128).T for v in (dw_b, ln_w, ln_b)], axis=1))
    relb = np.zeros((128, 8), np.float32)
    relb[:32] = rel_bias
    relb[32] = 1.0
    relb[64:] = relb[:64]
    oh = np.concatenate([_onehot_const(), _onehot_const()], axis=0)
    identf = np.eye(128, dtype=np.float32)

    def perm_cols(base):
        idx = np.arange(512).reshape(2, 4, 64).transpose(1, 0, 2).reshape(-1)
        return base + idx
    cols = np.arange(2816)
    cols[0:512] = perm_cols(0)
    cols[768:1280] = perm_cols(768)
    w_in_p = w_in[:, cols]
    w_in_tiled = np.ascontiguousarray(w_in_p.reshape(8, 128, 22, 128).transpose(2, 1, 0, 3)).reshape(22 * 128, 1024)
    rows = np.arange(1024)
    rows[0:512] = np.arange(512).reshape(2, 4, 64).transpose(1, 0, 2).reshape(-1)
    w_out_p = w_out[rows, :]
    w_out_tiled = np.ascontiguousarray(w_out_p.reshape(8, 128, 8, 128).transpose(2, 1, 0, 3)).reshape(8 * 128, 1024)

    in_maps = []
    for c in range(NCORES):
        b, half = c // 2, c % 2
        xin = np.zeros((NTOK, 1024), np.float32)
        if half == 1:
            xin[:HALO] = x[b, SEQ_PER_CORE - HALO:SEQ_PER_CORE]
        xin[HALO:] = x[b, half * SEQ_PER_CORE:(half + 1) * SEQ_PER_CORE]
        pm = np.full((128, 1), NEG if half == 0 else 0.0, np.float32)
        in_maps.append({"xin": xin, "w_in": w_in_tiled, "w_out": w_out_tiled, "nw": nw, "qkw": qkw, "sinks_t": sinks_t, "dww": dww,
                        "cvec": cvec, "relb": relb, "oh": oh, "identf": identf, "pm": pm})
    nc = build_program(debug=DEBUG)
    res = run_bass_kernel_spmd(nc, in_maps, core_ids=list(range(NCORES)))
    out = np.empty((4, 4096, 1024), np.float32)
    for c in range(NCORES):
        b, half = c // 2, c % 2
        out[b, half * SEQ_PER_CORE:(half + 1) * SEQ_PER_CORE] = res.results[c]["y"]
    if DEBUG:
        kernel.debug_results = res.results
    return out
```
